# Optimizing a Trainium2 kernel written in Bass

```python
import math
import jax, jax.numpy as jnp
from jax import lax
import numpy as np

D_MODEL = 1024
BATCH = 4
SEQ = 4096
DEPTH = 1
DEC_BATCH = 32
DEC_SEQ = 1
PAST_LEN = 16384
PAGE_SIZE = 128

D_MIX = D_MODEL
D_ATT = D_MIX // 2
D_SSM = D_MIX - D_ATT
HEAD_DIM = 64
N_ATT_HEADS = D_ATT // HEAD_DIM
SSM_HEAD_DIM = 64
N_SSM_HEADS = D_SSM // SSM_HEAD_DIM
N_SSM_GROUPS = 2
HEADS_PER_GROUP = N_SSM_HEADS // N_SSM_GROUPS
D_STATE = 128
CONV_W = 4
CONV_DIM = D_SSM + 2 * N_SSM_GROUPS * D_STATE
SSD_CHUNK = 128
Q_BLOCK = 128
D_FF = ((8 * D_MODEL // 3 + 127) // 128) * 128
FFN_RESIDUAL = 0.5
EPS = 1e-6
SPLITS = (D_ATT, 2 * D_ATT, 3 * D_ATT, 3 * D_ATT + N_ATT_HEADS,
          3 * D_ATT + N_ATT_HEADS + D_SSM, 3 * D_ATT + N_ATT_HEADS + D_SSM + CONV_DIM)
N_IN = SPLITS[-1] + N_SSM_HEADS

kernel_name = "hymba_fox_ssd_macaron_step"

F32 = jnp.float32


def rmsnorm(x, g):
    xf = x.astype(F32)
    y = xf * lax.rsqrt(jnp.mean(xf * xf, axis=-1, keepdims=True) + EPS) * g.astype(F32)
    return y.astype(x.dtype)


def group_rmsnorm(x, g, groups):
    shp = x.shape
    xf = x.astype(F32).reshape(*shp[:-1], groups, shp[-1] // groups)
    xf = xf * lax.rsqrt(jnp.mean(xf * xf, axis=-1, keepdims=True) + EPS)
    return xf.reshape(shp) * g.astype(F32)


def half_ffn(x, norm_g, w_in, w_out):
    h = rmsnorm(x, norm_g)
    gate, up = jnp.split(h @ w_in, 2, axis=-1)
    return x + FFN_RESIDUAL * ((jax.nn.silu(gate) * up) @ w_out)


def causal_conv(u, prefix, w, b):
    L = u.shape[1]
    cat = jnp.concatenate([prefix.astype(u.dtype), u], axis=1)
    out = b + sum(w[j] * cat[:, j:j + L] for j in range(CONV_W))
    return out, cat[:, -(CONV_W - 1):]


def mixer_inputs(h, conv_prefix, w_in, b_f, conv_w, conv_b, dt_bias, a_log):
    b, L, _ = h.shape
    q, k, v, f_logit, z, xbc, dt_raw = jnp.split(h @ w_in, list(SPLITS), axis=-1)
    q = q.reshape(b, L, N_ATT_HEADS, HEAD_DIM)
    k = k.reshape(b, L, N_ATT_HEADS, HEAD_DIM)
    v = v.reshape(b, L, N_ATT_HEADS, HEAD_DIM)
    logf = jax.nn.log_sigmoid((f_logit + b_f).astype(F32))
    xbc_c, conv_state = causal_conv(xbc, conv_prefix, conv_w, conv_b)
    xbc_c = jax.nn.silu(xbc_c)
    xs, Bm, Cm = jnp.split(xbc_c, [D_SSM, D_SSM + N_SSM_GROUPS * D_STATE], axis=-1)
    xs = xs.reshape(b, L, N_SSM_HEADS, SSM_HEAD_DIM).astype(F32)
    Bh = jnp.repeat(Bm.reshape(b, L, N_SSM_GROUPS, D_STATE), HEADS_PER_GROUP, axis=2).astype(F32)
    Ch = jnp.repeat(Cm.reshape(b, L, N_SSM_GROUPS, D_STATE), HEADS_PER_GROUP, axis=2).astype(F32)
    dt = jax.nn.softplus((dt_raw + dt_bias).astype(F32))
    A = -jnp.exp(a_log.astype(F32))
    return q, k, v, logf, z, xs, Bh, Ch, dt, A, conv_state


def fox_prompt(q, k, v, logf):
    b, L, H, hd = q.shape
    nb = L // Q_BLOCK
    scale = HEAD_DIM ** -0.5
    c = jnp.cumsum(logf, axis=1)
    cT = c.transpose(0, 2, 1)
    key_pos = jnp.arange(L)
    qb = q.reshape(b, nb, Q_BLOCK, H, hd).transpose(1, 0, 2, 3, 4)
    cb = c.reshape(b, nb, Q_BLOCK, H).transpose(1, 0, 3, 2)

    def block(args):
        qi, ci, i = args
        s = jnp.einsum('bqhd,bkhd->bhqk', qi, k).astype(F32) * scale
        s = s + ci[..., None] - cT[:, :, None, :]
        qpos = i * Q_BLOCK + jnp.arange(Q_BLOCK)
        s = jnp.where(key_pos[None, :] <= qpos[:, None], s, -jnp.inf)
        p = jax.nn.softmax(s, axis=-1)
        return jnp.einsum('bhqk,bkhd->bqhd', p.astype(v.dtype), v)

    o = lax.map(block, (qb, cb, jnp.arange(nb)))
    return o.transpose(1, 0, 2, 3, 4).reshape(b, L, H, hd)


def fox_sample(q, k_new, v_new, logf_new, k_past, v_past, logf_past):
    T = q.shape[1]
    P = k_past.shape[1]
    scale = HEAD_DIM ** -0.5
    rev = lax.cumsum(logf_past, axis=1, reverse=True) - logf_past
    cn = jnp.cumsum(logf_new, axis=1)
    cnT = cn.transpose(0, 2, 1)
    s_past = jnp.einsum('bqhd,bkhd->bhqk', q, k_past).astype(F32) * scale
    s_past = s_past + cnT[..., None] + rev.transpose(0, 2, 1)[:, :, None, :]
    s_new = jnp.einsum('bqhd,bkhd->bhqk', q, k_new).astype(F32) * scale
    s_new = s_new + cnT[..., None] - cnT[:, :, None, :]
    causal = jnp.arange(T)[None, :] <= jnp.arange(T)[:, None]
    s_new = jnp.where(causal, s_new, -jnp.inf)
    p = jax.nn.softmax(jnp.concatenate([s_past, s_new], axis=-1), axis=-1)
    o = jnp.einsum('bhqk,bkhd->bqhd', p[..., :P].astype(v_past.dtype), v_past)
    return o + jnp.einsum('bhqk,bkhd->bqhd', p[..., P:].astype(v_new.dtype), v_new)


def ssd_prompt(x, dt, A, Bh, Ch, d_skip):
    b, L, h, p = x.shape
    n = Bh.shape[-1]
    nc = L // SSD_CHUNK
    r = lambda t: t.reshape(b, nc, SSD_CHUNK, *t.shape[2:])
    xdt = r(x * dt[..., None])
    a_c = jnp.cumsum(r(dt * A), axis=2)
    Bc, Cc = r(Bh), r(Ch)
    seg = a_c[:, :, :, None, :] - a_c[:, :, None, :, :]
    tri = jnp.tril(jnp.ones((SSD_CHUNK, SSD_CHUNK), dtype=bool))
    Lm = jnp.exp(jnp.where(tri[None, None, :, :, None], seg, -jnp.inf))
    scores = jnp.einsum('bcthn,bcshn->bctsh', Cc, Bc) * Lm
    y_diag = jnp.einsum('bctsh,bcshp->bcthp', scores, xdt)
    decay_end = jnp.exp(a_c[:, :, -1:, :] - a_c)
    states = jnp.einsum('bcshn,bcsh,bcshp->bchpn', Bc, decay_end, xdt)
    chunk_decay = jnp.exp(a_c[:, :, -1, :])

    def step(hprev, inp):
        st, dec = inp
        return dec[:, :, None, None] * hprev + st, hprev

    h0 = jnp.zeros((b, h, p, n), F32)
    h_final, h_starts = lax.scan(step, h0, (states.transpose(1, 0, 2, 3, 4),
                                            chunk_decay.transpose(1, 0, 2)))
    h_starts = h_starts.transpose(1, 0, 2, 3, 4)
    y_off = jnp.einsum('bcthn,bchpn,bcth->bcthp', Cc, h_starts, jnp.exp(a_c))
    y = (y_diag + y_off).reshape(b, L, h, p) + d_skip.astype(F32)[:, None] * x
    return y, h_final


def ssd_sample(x, dt, A, Bh, Ch, d_skip, h0):
    def step(hs, inp):
        xt, dtt, Bt, Ct = inp
        hs = jnp.exp(dtt * A)[..., None, None] * hs + jnp.einsum('bhp,bhn->bhpn', xt * dtt[..., None], Bt)
        return hs, jnp.einsum('bhn,bhpn->bhp', Ct, hs)

    h_final, ys = lax.scan(step, h0.astype(F32), (x.swapaxes(0, 1), dt.swapaxes(0, 1),
                                                  Bh.swapaxes(0, 1), Ch.swapaxes(0, 1)))
    y = ys.swapaxes(0, 1) + d_skip.astype(F32)[:, None] * x
    return y, h_final


def mixer_output(att_o, y_ssm, z, att_norm, ssm_norm, w_out):
    b, L = z.shape[:2]
    att = rmsnorm(att_o.reshape(b, L, D_ATT), att_norm)
    g = y_ssm.reshape(b, L, D_SSM) * jax.nn.silu(z.astype(F32))
    g = group_rmsnorm(g, ssm_norm, N_SSM_GROUPS)
    mixed = jnp.concatenate([att.astype(z.dtype), g.astype(z.dtype)], axis=-1)
    return mixed @ w_out


def setup_inputs(seed: int = 0) -> dict:
    key = jax.random.key(seed)
    ks = jax.random.split(key, 32)
    n_pages = PAST_LEN // PAGE_SIZE
    n_used = DEC_BATCH * n_pages
    n_pool = n_used + max(1, n_used // 4)
    nrm = lambda k, shape, s=1.0: jax.random.normal(k, shape, F32) * s
    gain = lambda k, dim: 1.0 + 0.02 * jax.random.normal(k, (DEPTH, dim), F32)
    dt0 = jnp.exp(jax.random.uniform(ks[10], (DEPTH, N_SSM_HEADS), F32,
                                     math.log(1e-3), math.log(1e-1)))
    page_table = jax.random.permutation(ks[7], n_pool)[:n_used].reshape(DEC_BATCH, n_pages).astype(jnp.int32)
    b_f = jnp.linspace(1.0, 5.0, N_ATT_HEADS, dtype=F32)[None, :] + 0.1 * nrm(ks[9], (DEPTH, N_ATT_HEADS))
    logf_bias = jnp.linspace(6.0, 14.0, N_ATT_HEADS, dtype=F32)
    cache_logf = jax.nn.log_sigmoid(logf_bias + 0.5 * nrm(ks[4], (DEPTH, n_pool, PAGE_SIZE, N_ATT_HEADS)))
    return {
        "x_prompt": nrm(ks[0], (BATCH, SEQ, D_MODEL)),
        "x_sample": nrm(ks[1], (DEC_BATCH, DEC_SEQ, D_MODEL)),
        "cache_k": nrm(ks[2], (DEPTH, n_pool, PAGE_SIZE, N_ATT_HEADS, HEAD_DIM)),
        "cache_v": nrm(ks[3], (DEPTH, n_pool, PAGE_SIZE, N_ATT_HEADS, HEAD_DIM)),
        "cache_logf": cache_logf,
        "page_table": page_table,
        "state_conv": nrm(ks[5], (DEPTH, DEC_BATCH, CONV_W - 1, CONV_DIM)),
        "state_ssm": nrm(ks[6], (DEPTH, DEC_BATCH, N_SSM_HEADS, SSM_HEAD_DIM, D_STATE), 0.1),
        "ffn1_norm": gain(ks[8], D_MODEL),
        "w_ffn1_in": nrm(ks[11], (DEPTH, D_MODEL, 2 * D_FF), D_MODEL ** -0.5),
        "w_ffn1_out": nrm(ks[12], (DEPTH, D_FF, D_MODEL), D_FF ** -0.5),
        "mix_norm": gain(ks[13], D_MODEL),
        "w_in": nrm(ks[14], (DEPTH, D_MODEL, N_IN), D_MODEL ** -0.5),
        "b_f": b_f,
        "conv_w": nrm(ks[15], (DEPTH, CONV_W, CONV_DIM), CONV_W ** -0.5),
        "conv_b": nrm(ks[16], (DEPTH, CONV_DIM), 0.02),
        "dt_bias": dt0 + jnp.log(-jnp.expm1(-dt0)),
        "a_log": jnp.log(jax.random.uniform(ks[17], (DEPTH, N_SSM_HEADS), F32, 1.0, 16.0)),
        "d_skip": 1.0 + 0.02 * nrm(ks[18], (DEPTH, N_SSM_HEADS)),
        "att_out_norm": gain(ks[19], D_ATT),
        "ssm_out_norm": gain(ks[20], D_SSM),
        "w_out": nrm(ks[21], (DEPTH, D_MIX, D_MODEL), D_MIX ** -0.5),
        "ffn2_norm": gain(ks[22], D_MODEL),
        "w_ffn2_in": nrm(ks[23], (DEPTH, D_MODEL, 2 * D_FF), D_MODEL ** -0.5),
        "w_ffn2_out": nrm(ks[24], (DEPTH, D_FF, D_MODEL), D_FF ** -0.5),
        "final_norm": 1.0 + 0.02 * nrm(ks[25], (D_MODEL,)),
    }


def reference(x_prompt, x_sample, cache_k, cache_v, cache_logf, page_table, state_conv, state_ssm,
              ffn1_norm, w_ffn1_in, w_ffn1_out, mix_norm, w_in, b_f, conv_w, conv_b, dt_bias, a_log,
              d_skip, att_out_norm, ssm_out_norm, w_out, ffn2_norm, w_ffn2_in, w_ffn2_out, final_norm):
    n_pages = page_table.shape[1]
    past_len = n_pages * PAGE_SIZE
    dec_b = x_sample.shape[0]
    xp, xs = x_prompt, x_sample
    kp_l, vp_l, fp_l, cp_l, sp_l = [], [], [], [], []
    ks_l, vs_l, fs_l, cs_l, ss_l = [], [], [], [], []
    for l in range(DEPTH):
        xp = half_ffn(xp, ffn1_norm[l], w_ffn1_in[l], w_ffn1_out[l])
        xs = half_ffn(xs, ffn1_norm[l], w_ffn1_in[l], w_ffn1_out[l])
        mix_w = (w_in[l], b_f[l], conv_w[l], conv_b[l], dt_bias[l], a_log[l])
        hp = rmsnorm(xp, mix_norm[l])
        zero_prefix = jnp.zeros((xp.shape[0], CONV_W - 1, CONV_DIM), xp.dtype)
        q, k, v, logf, z, xss, Bh, Ch, dt, A, conv_p = mixer_inputs(hp, zero_prefix, *mix_w)
        att = fox_prompt(q, k, v, logf)
        y_ssm, ssm_p = ssd_prompt(xss, dt, A, Bh, Ch, d_skip[l])
        xp = xp + mixer_output(att, y_ssm, z, att_out_norm[l], ssm_out_norm[l], w_out[l])
        kp_l.append(k); vp_l.append(v); fp_l.append(logf); cp_l.append(conv_p); sp_l.append(ssm_p)
        hs = rmsnorm(xs, mix_norm[l])
        q2, k2, v2, logf2, z2, xss2, Bh2, Ch2, dt2, A2, conv_s = mixer_inputs(hs, state_conv[l], *mix_w)
        k_past = cache_k[l][page_table].reshape(dec_b, past_len, N_ATT_HEADS, HEAD_DIM)
        v_past = cache_v[l][page_table].reshape(dec_b, past_len, N_ATT_HEADS, HEAD_DIM)
        f_past = cache_logf[l][page_table].reshape(dec_b, past_len, N_ATT_HEADS).astype(F32)
        att2 = fox_sample(q2, k2, v2, logf2, k_past, v_past, f_past)
        y_ssm2, ssm_s = ssd_sample(xss2, dt2, A2, Bh2, Ch2, d_skip[l], state_ssm[l])
        xs = xs + mixer_output(att2, y_ssm2, z2, att_out_norm[l], ssm_out_norm[l], w_out[l])
        ks_l.append(k2); vs_l.append(v2); fs_l.append(logf2); cs_l.append(conv_s); ss_l.append(ssm_s)
        xp = half_ffn(xp, ffn2_norm[l], w_ffn2_in[l], w_ffn2_out[l])
        xs = half_ffn(xs, ffn2_norm[l], w_ffn2_in[l], w_ffn2_out[l])
    y_prompt = rmsnorm(xp, final_norm)
    y_sample = rmsnorm(xs, final_norm)
    k_prompt = jnp.stack(kp_l); v_prompt = jnp.stack(vp_l); logf_prompt = jnp.stack(fp_l)
    conv_prompt = jnp.stack(cp_l); ssm_prompt = jnp.stack(sp_l)
    k_sample = jnp.stack(ks_l); v_sample = jnp.stack(vs_l); logf_sample = jnp.stack(fs_l)
    conv_sample = jnp.stack(cs_l); ssm_sample = jnp.stack(ss_l)
    return (y_prompt, y_sample, k_prompt, v_prompt, logf_prompt, conv_prompt, ssm_prompt,
            k_sample, v_sample, logf_sample, conv_sample, ssm_sample)
```

```python
import numpy as np
import ml_dtypes
from contextlib import ExitStack
import concourse.bass as bass
import concourse.mybir as mybir
from concourse.bass_utils import run_bass_kernel_spmd

F32 = mybir.dt.float32
BF16 = mybir.dt.bfloat16
I32 = mybir.dt.int32
AF = mybir.ActivationFunctionType
ALU = mybir.AluOpType
AX = mybir.AxisListType

D = 1024
DFF = 2816
FC = 22
NIN = 3088
NH = 8
HD = 64
EPS = 1e-6
SCALE = 0.125
NS = 68
NEG = -30000.0
NDS = 8
PG = 32


class Buf:
    __slots__ = ("name", "w", "r", "excl")

    def __init__(self, name=""):
        self.name = name
        self.w = None
        self.r = {}
        self.excl = False


class T:
    __slots__ = ("t", "B")

    def __init__(self, t, name=""):
        self.t = t
        self.B = Buf(name)


class Ring:
    def __init__(self, items):
        self.items = items
        self.i = 0

    def next(self):
        it = self.items[self.i]
        self.i = (self.i + 1) % len(self.items)
        return it


class KB:
    def __init__(self, nc):
        self.nc = nc
        self.E = {"pe": nc.tensor, "act": nc.scalar, "dve": nc.vector, "pool": nc.gpsimd, "sp": nc.sync}
        self.semh = {}
        self.cnt = {}
        for e in ("pe", "act", "dve", "pool"):
            self.semh["e:" + e] = nc.alloc_semaphore("es_" + e)
            self.cnt[e] = 0
        self.seen = {e: {} for e in self.E}
        self.dval = {}
        self.dnext = {}
        for q in ("sp", "pool", "act"):
            self.dval[q] = [0] * NDS
            self.dnext[q] = 0
            for i in range(NDS):
                self.semh[f"d:{q}:{i}"] = nc.alloc_semaphore(f"ds_{q}{i}")
        self.semh["cc"] = nc.alloc_semaphore("cc")
        self.ccval = 0
        self.bufs = {}
        ps = [nc.alloc_psum_tensor(f"psb{i}", [128, 512], F32) for i in range(8)]
        self.psring = Ring([T(p, f"ps{i}") for i, p in enumerate(ps)])
        for t in self.psring.items:
            t.B.excl = True
        self.held = set()

    def B(self, *key):
        b = self.bufs.get(key)
        if b is None:
            b = Buf(str(key))
            self.bufs[key] = b
        return b

    def ps(self):
        for _ in range(16):
            t = self.psring.next()
            if id(t) not in self.held:
                return t
        raise RuntimeError("no psum bank")

    def ps_hold(self):
        t = self.ps()
        self.held.add(id(t))
        return t

    def ps_release(self, t):
        self.held.discard(id(t))

    def _deps(self, reads, writes):
        evs = []
        for b in reads:
            if b.w:
                evs.append(b.w)
        for b in writes:
            if b.w:
                evs.append(b.w)
            evs.extend(b.r.items())
        return evs

    def _wait(self, eng, evs):
        need = {}
        for (k, v) in evs:
            if k == "e:pe" and eng == "pe":
                continue
            if need.get(k, 0) < v:
                need[k] = v
        for k, v in need.items():
            if self.seen[eng].get(k, 0) >= v:
                continue
            self.E[eng].wait_ge(self.semh[k], v)
            self.seen[eng][k] = v

    def _mark(self, ev, reads, writes):
        for b in reads:
            if b.r.get(ev[0], 0) < ev[1]:
                b.r[ev[0]] = ev[1]
        for b in writes:
            b.w = ev
            b.r = {}

    def op(self, eng, fn, reads=(), writes=()):
        ex = [b for b in reads if b.excl]
        if ex:
            reads = [b for b in reads if not b.excl]
            writes = list(writes) + ex
        self._wait(eng, self._deps(reads, writes))
        inst = fn(self.E[eng])
        self.cnt[eng] += 1
        inst.then_inc(self.semh["e:" + eng], 1)
        ev = ("e:" + eng, self.cnt[eng])
        self._mark(ev, reads, writes)
        return ev

    def dma(self, q, out=None, in_=None, reads=(), writes=(), fn=None, **kw):
        evs = self._deps(reads, writes)
        i = self.dnext[q]
        self.dnext[q] = (i + 1) % NDS
        key = f"d:{q}:{i}"
        if self.dval[q][i] > 0:
            evs.append((key, self.dval[q][i]))
        self._wait(q, evs)
        if fn is None:
            inst = self.E[q].dma_start(out=out, in_=in_, **kw)
        else:
            inst = fn(self.E[q])
        self.dval[q][i] += 16
        inst.then_inc(self.semh[key], 16)
        ev = (key, self.dval[q][i])
        self._mark(ev, reads, writes)
        return ev

    def allgather(self, src, dst, groups, reads=(), writes=()):
        self._wait("pool", self._deps(reads, writes))
        inst = self.nc.gpsimd.collective_compute("AllGather", ALU.bypass, replica_groups=groups,
                                                 ins=[src], outs=[dst])
        self.ccval += 1
        inst.then_inc(self.semh["cc"])
        ev = ("cc", self.ccval)
        self._mark(ev, reads, writes)
        return ev

    def all_events(self):
        evs = [("e:" + e, c) for e, c in self.cnt.items() if c > 0]
        for q in self.dval:
            for i, v in enumerate(self.dval[q]):
                if v > 0:
                    evs.append((f"d:{q}:{i}", v))
        if self.ccval:
            evs.append(("cc", self.ccval))
        return evs

    def barrier(self, engines=("pe", "act", "dve", "pool", "sp")):
        evs = self.all_events()
        for e in engines:
            self._wait(e, [ev for ev in evs if not (ev[0] == "e:" + e)])


def rawap(ap, off, pat):
    return bass.AP(tensor=ap.tensor, offset=ap.offset + off, ap=[list(p) for p in pat])


def mmg(e, out, items):
    n = len(items)
    inst = None
    for i, (l, r) in enumerate(items):
        inst = e.matmul(out, lhsT=l, rhs=r, start=(i == 0), stop=(i == n - 1))
    return inst


class Ctx:
    pass


def build(NT, NPGS, NPOOL, stages=99, debug=False, ncores=8, cut=99, skip=None):
    nc = bass.Bass("TRN2", target_bir_lowering=False, num_devices=ncores)
    kb = KB(nc)
    NTOT = NT + NS
    NTT = NT // 512
    NG = NPGS // PG
    cx = Ctx()
    cx.nc, cx.kb, cx.NT, cx.NTOT, cx.NTT, cx.NG, cx.NPGS = nc, kb, NT, NTOT, NTT, NG, NPGS
    cx.cut = cut

    def din(name, shape, dt=F32):
        return nc.dram_tensor(name, list(shape), dt, kind="ExternalInput").ap()

    def dout(name, shape, dt=F32):
        return nc.dram_tensor(name, list(shape), dt, kind="ExternalOutput").ap()

    def dscr(name, shape, dt=F32):
        return nc.dram_tensor(name, list(shape), dt, kind=("ExternalOutput" if (debug and name in debug) else "Internal")).ap()

    I = {}
    I["xin"] = din("xin", [NTOT, D])
    I["ck"] = din("ck", [NPOOL * 4, 2048])
    I["cv"] = din("cv", [NPOOL * 4, 2048])
    I["clf"] = din("clf", [NPOOL * 4, 32])
    I["ptab"] = din("ptab", [32, NPGS], I32)
    I["sconv"] = din("sconv", [4, 3, D])
    I["sssm"] = din("sssm", [128, 2048])
    for nm in ("ffn1_norm", "mix_norm", "ffn2_norm", "final_norm", "conv_b"):
        I[nm] = din(nm, [1, D])
    I["w_ffn1_in"] = din("w_ffn1_in", [D, 2 * DFF])
    I["w_ffn2_in"] = din("w_ffn2_in", [D, 2 * DFF])
    I["w_ffn1_out"] = din("w_ffn1_out", [DFF, D])
    I["w_ffn2_out"] = din("w_ffn2_out", [DFF, D])
    I["w_in"] = din("w_in", [D, NIN])
    I["w_out"] = din("w_out", [D, D])
    I["wc"] = din("wc", [D, 200])
    I["conv_w"] = din("conv_w", [4, D])
    for nm in ("b_f", "dt_bias", "a_log", "d_skip"):
        I[nm] = din(nm, [1, 8])
    I["att_out_norm"] = din("att_out_norm", [1, 512])
    I["ssm_out_norm"] = din("ssm_out_norm", [1, 512])
    I["percore"] = din("percore", [128, 8])
    I["sel"] = din("sel", [32, 4])
    I["cst_idf"] = din("cst_idf", [128, 128])
    I["cst_tri"] = din("cst_tri", [128, 128])
    I["cst_ustrict"] = din("cst_ustrict", [128, 128])
    I["cst_maskneg"] = din("cst_maskneg", [128, 4, 512])
    I["cst_maskbd"] = din("cst_maskbd", [32, 32, 65])
    I["cst_selcol"] = din("cst_selcol", [128, 1024])
    I["cst_uord"] = din("cst_uord", [128, 128])
    cx.I = I

    O = {}
    O["y_prompt"] = dout("y_prompt", [NT, D])
    O["y_sample"] = dout("y_sample", [4, D])
    O["k_prompt"] = dout("k_prompt", [NT, 512])
    O["v_prompt"] = dout("v_prompt", [NT, 512])
    O["logf_prompt"] = dout("logf_prompt", [NT, 8])
    O["conv_prompt"] = dout("conv_prompt", [3, D])
    O["ssm_prompt"] = dout("ssm_prompt", [512, 128])
    O["k_sample"] = dout("k_sample", [32, 512])
    O["v_sample"] = dout("v_sample", [32, 512])
    O["logf_sample"] = dout("logf_sample", [32, 8])
    O["conv_sample"] = dout("conv_sample", [4, 3, D])
    O["ssm_sample"] = dout("ssm_sample", [128, 2048])
    cx.O = O

    Sx = {}
    Sx["h0T"] = dscr("h0T", [D, NTOT], BF16)
    Sx["aT1"] = dscr("aT1", [DFF, NTOT], BF16)
    Sx["x1"] = dscr("x1", [NTOT, D])
    Sx["h1T"] = dscr("h1T", [D, NTOT], BF16)
    Sx["QT"] = dscr("QT", [8 * 65, NT], BF16)
    Sx["KT"] = dscr("KT", [512, NT], BF16)
    Sx["KTg"] = dscr("KTg", [1024, NT], BF16)
    Sx["Vb"] = dscr("Vb", [NT, 512], BF16)
    Sx["Vbg"] = dscr("Vbg", [2 * NT, 512], BF16)
    Sx["CL"] = dscr("CL", [8, NT])
    Sx["CLg"] = dscr("CLg", [16, NT])
    Sx["sz"] = dscr("sz", [NT + 4, 512], BF16)
    Sx["xs"] = dscr("xs", [NT, 512], BF16)
    Sx["Btok"] = dscr("Btok", [NT, 256], BF16)
    Sx["BT"] = dscr("BT", [256, NT], BF16)
    Sx["CT"] = dscr("CT", [256, NT], BF16)
    Sx["dt"] = dscr("dt", [NT, 8])
    Sx["att"] = dscr("att", [NT + 4, 512])
    Sx["yssm"] = dscr("yssm", [NT + 4, 512])
    Sx["st1"] = dscr("st1", [128, 512])
    Sx["st1g"] = dscr("st1g", [256, 512])
    Sx["x2"] = dscr("x2", [NT + 4, D])
    Sx["h2T"] = dscr("h2T", [D, NTOT], BF16)
    Sx["aT2"] = dscr("aT2", [DFF, NTOT], BF16)
    Sx["own"] = dscr("own", [4, NIN])
    Sx["ssd2"] = dscr("ssd2", [4, 2048])
    Sx["y2"] = dscr("y2", [4, 512])
    Sx["att2"] = dscr("att2", [32, 64])
    Sx["att2g"] = dscr("att2g", [256, 64])
    Sx["mix2"] = dscr("mix2", [4, D])
    cx.S = Sx

    with ExitStack() as gst:
        def sb(name, shape, dt=F32, st=gst):
            return T(st.enter_context(nc.sbuf_tensor(name, list(shape), dt)), name)
        cx.sb = sb
        cx.idf = sb("idf", [128, 128])
        cx.idb = sb("idb", [128, 128], BF16)
        cx.cst = sb("cst", [128, 8])
        cx.pc = sb("pc", [128, 8])
        cx.qkv = sb("qkv", [32, 200])
        kb.dma("sp", out=cx.idf.t[:], in_=I["cst_idf"], writes=[cx.idf.B])
        kb.dma("pool", out=cx.idb.t[:], in_=I["cst_idf"], writes=[cx.idb.B])
        kb.dma("sp", out=cx.pc.t[:], in_=I["percore"], writes=[cx.pc.B])
        kb.op("dve", lambda e: e.memset(cx.cst.t[:, 0:1], 1.0), writes=[cx.cst.B])
        kb.op("dve", lambda e: e.memset(cx.cst.t[:, 1:2], 0.0), writes=[cx.cst.B])

        phase_prenorm(cx, I["xin"], I["ffn1_norm"], Sx["h0T"], "p0")
        if stages >= 1:
            phase_ffn_in(cx, Sx["h0T"], I["w_ffn1_in"], Sx["aT1"], "p1")
        if stages >= 2:
            phase_ffn_out(cx, I["xin"], Sx["aT1"], I["w_ffn1_out"], "p2a", mode="mid")
        if stages >= 3:
            phase_proj(cx)
        if stages >= 4 and not (skip and "sample" in skip):
            phase_sample(cx)
        if stages >= 4 and cut >= 90:
            phase_ssd(cx, 1)
        if stages >= 5:
            phase_attn(cx)
        if stages >= 6:
            phase_ssd(cx, 2)
        if stages >= 7:
            phase_mix(cx)
        if stages >= 8:
            phase_ffn_in(cx, Sx["h2T"], I["w_ffn2_in"], Sx["aT2"], "p4", small=4)
        if stages >= 9:
            phase_ffn_out(cx, Sx["x2"], Sx["aT2"], I["w_ffn2_out"], "p5", mode="final")
        kb.barrier()
    return nc


def token_tiles(cx, small_rows=NS):
    tiles = []
    for tt in range(cx.NTT):
        tiles.append((tt * 512, [(tt * 512 + s * 128, 128) for s in range(4)]))
    if small_rows:
        tiles.append((cx.NT, [(cx.NT, small_rows)]))
    return tiles


def load_gain(cx, st, g_ap, n, name):
    gb = cx.sb(name, [128, n], F32, st)
    cx.kb.dma("sp", out=gb.t[:], in_=rawap(g_ap, 0, [[0, 128], [1, n]]), writes=[gb.B])
    return gb


def norm_to_T(cx, xt, p, n, gb, rs, hT, col):
    kb = cx.kb
    junk, stat, h = rs["junk"].next(), rs["stat"].next(), rs["h"].next()
    nk = n // 128
    kb.op("act", lambda e: e.activation(out=junk.t[:p, :n], in_=xt.t[:p, :n], func=AF.Square,
                                        accum_out=stat.t[:p, 0:1]), reads=[xt.B], writes=[junk.B, stat.B])
    kb.op("dve", lambda e: e.tensor_scalar(out=stat.t[:p, 1:2], in0=stat.t[:p, 0:1], scalar1=1.0 / n, scalar2=EPS,
                                           op0=ALU.mult, op1=ALU.add), reads=[stat.B], writes=[stat.B])
    kb.op("act", lambda e: e.activation(out=stat.t[:p, 2:3], in_=stat.t[:p, 1:2], func=AF.Sqrt),
          reads=[stat.B], writes=[stat.B])
    kb.op("dve", lambda e: e.reciprocal(out=stat.t[:p, 3:4], in_=stat.t[:p, 2:3]), reads=[stat.B], writes=[stat.B])
    kb.op("dve", lambda e: e.scalar_tensor_tensor(out=h.t[:p, :n], in0=xt.t[:p, :n], scalar=stat.t[:p, 3:4],
                                                  in1=gb.t[:p, :n], op0=ALU.mult, op1=ALU.mult),
          reads=[xt.B, stat.B, gb.B], writes=[h.B])
    ps = kb.ps()
    psb = ps.t[:].bitcast(BF16)

    def tr(e):
        inst = None
        for k in range(nk):
            inst = e.transpose(psb[:, k * 128:k * 128 + p], h.t[:p, k * 128:(k + 1) * 128], cx.idb.t[:p, :p])
        return inst
    kb.op("pe", tr, reads=[h.B, cx.idb.B], writes=[ps.B])
    kb.op("act", lambda e: e.copy(out=hT.t[:, 0:nk, col:col + p],
                                  in_=psb.rearrange("q (k c) -> q k c", c=128)[:, 0:nk, 0:p]),
          reads=[ps.B], writes=[hT.B])
    return stat


def norm_rings(cx, st, tag, n=D):
    return {
        "junk": Ring([cx.sb(f"junk{tag}", [128, n], BF16, st)]),
        "stat": Ring([cx.sb(f"stat{tag}{i}", [128, 8], F32, st) for i in range(4)]),
        "h": Ring([cx.sb(f"hn{tag}{i}", [128, n], BF16, st) for i in range(2)]),
    }


def phase_prenorm(cx, x_ap, g_ap, hT_dst, tag):
    kb, nc = cx.kb, cx.nc
    with ExitStack() as st:
        gb = load_gain(cx, st, g_ap, D, "gb" + tag)
        rs = norm_rings(cx, st, tag)
        xr = Ring([cx.sb(f"x{tag}{i}", [128, D], F32, st) for i in range(3)])
        hTr = Ring([cx.sb(f"hT{tag}{i}", [128, 8, 512], BF16, st) for i in range(2)])
        for (c0, subs) in token_tiles(cx):
            hT = hTr.next()
            col = 0
            for (r0, p) in subs:
                xt = xr.next()
                kb.dma("sp", out=xt.t[:p, :], in_=x_ap[r0:r0 + p, :], writes=[xt.B])
                norm_to_T(cx, xt, p, D, gb, rs, hT, col)
                col += p
            kb.dma("sp", out=hT_dst.rearrange("(k q) n -> q k n", q=128)[:, :, c0:c0 + col], in_=hT.t[:, :, 0:col],
                   reads=[hT.B], writes=[kb.B(tag, "hT", c0)])
        kb.barrier()


def phase_ffn_in(cx, hT_src, w_ap, aT_dst, tag, small=NS):
    kb, nc = cx.kb, cx.nc
    with ExitStack() as st:
        w1 = st.enter_context(nc.sbuf_tensor("w1" + tag, [128, 8, 2 * DFF], BF16))
        w1B = [Buf() for _ in range(8)]
        for k in range(8):
            for hf in range(2):
                kb.dma("pool", out=w1[:, k, hf * DFF:(hf + 1) * DFF],
                       in_=w_ap[k * 128:(k + 1) * 128, hf * DFF:(hf + 1) * DFF], writes=[w1B[k]])
        hTr = Ring([cx.sb(f"hT{tag}{i}", [128, 8, 512], BF16, st) for i in range(2)])
        sgr = Ring([cx.sb(f"sg{tag}{i}", [128, 512], F32, st) for i in range(3)])
        ar = Ring([cx.sb(f"a{tag}{i}", [128, FC, 512], BF16, st) for i in range(2)])
        for (c0, subs) in token_tiles(cx, small):
            N = sum(p for _, p in subs)
            hT = hTr.next()
            kb.dma("sp", out=hT.t[:, :, 0:N], in_=hT_src.rearrange("(k q) n -> q k n", q=128)[:, :, c0:c0 + N],
                   reads=[kb.B("all")], writes=[hT.B])
            a = ar.next()
            for f in range(FC):
                psg, psu = kb.ps(), kb.ps()
                kb.op("pe", lambda e: mmg(e, psg.t[:, :N], [(w1[:, k, f * 128:(f + 1) * 128], hT.t[:, k, :N]) for k in range(8)]),
                      reads=w1B + [hT.B], writes=[psg.B])
                kb.op("pe", lambda e: mmg(e, psu.t[:, :N], [(w1[:, k, DFF + f * 128:DFF + (f + 1) * 128], hT.t[:, k, :N]) for k in range(8)]),
                      reads=w1B + [hT.B], writes=[psu.B])
                sg = sgr.next()
                kb.op("act", lambda e: e.activation(out=sg.t[:, :N], in_=psg.t[:, :N], func=AF.Silu),
                      reads=[psg.B], writes=[sg.B])
                kb.op("dve", lambda e: e.tensor_tensor(out=a.t[:, f, :N], in0=psu.t[:, :N], in1=sg.t[:, :N], op=ALU.mult),
                      reads=[psu.B, sg.B], writes=[a.B])
            kb.dma("sp", out=aT_dst.rearrange("(f q) n -> q f n", q=128)[:, :, c0:c0 + N], in_=a.t[:, :, 0:N],
                   reads=[a.B], writes=[kb.B(tag, "aT", c0)])
        kb.barrier()


def phase_ffn_out(cx, x_ap, aT_src, w_ap, tag, mode):
    kb, nc, I, Sx, O = cx.kb, cx.nc, cx.I, cx.S, cx.O
    with ExitStack() as st:
        w2 = st.enter_context(nc.sbuf_tensor("w2" + tag, [128, FC, D], BF16))
        w2B = [Buf() for _ in range(FC)]
        for f in range(FC):
            kb.dma("pool", out=w2[:, f, :], in_=w_ap[f * 128:(f + 1) * 128, :], writes=[w2B[f]])
        gb = load_gain(cx, st, I["mix_norm"] if mode == "mid" else I["final_norm"], D, "gb" + tag)
        rs = norm_rings(cx, st, tag)
        xr = Ring([cx.sb(f"x{tag}{i}", [128, D], F32, st) for i in range(2)])
        x1r = Ring([cx.sb(f"xn{tag}{i}", [128, D], F32, st) for i in range(2)])
        ar = Ring([cx.sb(f"a{tag}{i}", [128, FC, 512], BF16, st) for i in range(2)])
        hTr = Ring([cx.sb(f"hT{tag}{i}", [128, 8, 512], BF16, st) for i in range(2)])
        yr = Ring([cx.sb(f"y{tag}{i}", [128, D], F32, st) for i in range(2)])
        small = NS if mode == "mid" else 4
        for (c0, subs) in token_tiles(cx, small):
            N = sum(p for _, p in subs)
            a = ar.next()
            kb.dma("sp", out=a.t[:, :, 0:N], in_=aT_src.rearrange("(f q) n -> q f n", q=128)[:, :, c0:c0 + N],
                   reads=[kb.B("all")], writes=[a.B])
            hT = hTr.next()
            col = 0
            for (r0, p) in subs:
                xt = xr.next()
                kb.dma("sp", out=xt.t[:p, :], in_=x_ap[r0:r0 + p, :], reads=[kb.B("all")], writes=[xt.B])
                xn = x1r.next()
                for hf in range(2):
                    ps = kb.ps()
                    kb.op("pe", lambda e: mmg(e, ps.t[:p, :], [(a.t[:, f, col:col + p], w2[:, f, hf * 512:(hf + 1) * 512]) for f in range(FC)]),
                          reads=w2B + [a.B], writes=[ps.B])
                    kb.op("dve", lambda e: e.scalar_tensor_tensor(out=xn.t[:p, hf * 512:(hf + 1) * 512], in0=ps.t[:p, :], scalar=0.5,
                                                                  in1=xt.t[:p, hf * 512:(hf + 1) * 512], op0=ALU.mult, op1=ALU.add),
                          reads=[ps.B, xt.B], writes=[xn.B])
                if mode == "mid":
                    kb.dma("sp", out=Sx["x1"][r0:r0 + p, :], in_=xn.t[:p, :], reads=[xn.B], writes=[kb.B(tag, "x1", r0)])
                    norm_to_T(cx, xn, p, D, gb, rs, hT, col)
                else:
                    junk, stat = rs["junk"].next(), rs["stat"].next()
                    kb.op("act", lambda e: e.activation(out=junk.t[:p, :], in_=xn.t[:p, :], func=AF.Square,
                                                        accum_out=stat.t[:p, 0:1]), reads=[xn.B], writes=[junk.B, stat.B])
                    kb.op("dve", lambda e: e.tensor_scalar(out=stat.t[:p, 1:2], in0=stat.t[:p, 0:1], scalar1=1.0 / D, scalar2=EPS,
                                                           op0=ALU.mult, op1=ALU.add), reads=[stat.B], writes=[stat.B])
                    kb.op("act", lambda e: e.activation(out=stat.t[:p, 2:3], in_=stat.t[:p, 1:2], func=AF.Sqrt),
                          reads=[stat.B], writes=[stat.B])
                    kb.op("dve", lambda e: e.reciprocal(out=stat.t[:p, 3:4], in_=stat.t[:p, 2:3]), reads=[stat.B], writes=[stat.B])
                    y = yr.next()
                    kb.op("dve", lambda e: e.scalar_tensor_tensor(out=y.t[:p, :], in0=xn.t[:p, :], scalar=stat.t[:p, 3:4],
                                                                  in1=gb.t[:p, :], op0=ALU.mult, op1=ALU.mult),
                          reads=[xn.B, stat.B, gb.B], writes=[y.B])
                    dst = O["y_prompt"][r0:r0 + p, :] if r0 < cx.NT else O["y_sample"][0:p, :]
                    kb.dma("sp", out=dst, in_=y.t[:p, :], reads=[y.B], writes=[kb.B(tag, "y", r0)])
                col += p
            if mode == "mid":
                kb.dma("sp", out=Sx["h1T"].rearrange("(k q) n -> q k n", q=128)[:, :, c0:c0 + col], in_=hT.t[:, :, 0:col],
                       reads=[hT.B], writes=[kb.B(tag, "h1T", c0)])
        kb.barrier()


def copy_op(kb, eng, out, in_, reads, writes):
    if eng == "act":
        return kb.op("act", lambda e: e.copy(out=out, in_=in_), reads=reads, writes=writes)
    return kb.op(eng, lambda e: e.tensor_copy(out=out, in_=in_), reads=reads, writes=writes)


def logsig_chain(kb, cx, x_ap, p, n, tmp, out_neg, rd, wr, sign):
    kb.op("act", lambda e: e.activation(out=tmp, in_=x_ap, func=AF.Exp, scale=float(sign)), reads=rd, writes=wr)
    kb.op("act", lambda e: e.activation(out=tmp, in_=tmp, func=AF.Ln, bias=cx.cst.t[:p, 0:1], scale=1.0),
          reads=wr + [cx.cst.B], writes=wr)


def phase_proj(cx):
    kb, nc, I, Sx, O = cx.kb, cx.nc, cx.I, cx.S, cx.O
    NT = cx.NT
    with ExitStack() as st:
        sb = lambda name, shape, dt=F32: cx.sb(name, shape, dt, st)
        win = st.enter_context(nc.sbuf_tensor("win", [128, 8, NIN], BF16))
        winB = [Buf() for _ in range(8)]
        for k in range(8):
            kb.dma("pool", out=win[:, k, :], in_=I["w_in"][k * 128:(k + 1) * 128, :], writes=[winB[k]])
        wc = sb("wcs", [128, 8, 200], BF16)
        for k in range(8):
            kb.dma("pool", out=wc.t[:, k, :], in_=I["wc"][k * 128:(k + 1) * 128, :], writes=[wc.B])
        nbf = sb("nbf", [8, 1])
        kb.dma("sp", out=nbf.t[:], in_=rawap(I["b_f"], 0, [[1, 8], [1, 1]]), writes=[nbf.B])
        kb.op("dve", lambda e: e.tensor_scalar(out=nbf.t[:], in0=nbf.t[:], scalar1=-1.0, scalar2=None, op0=ALU.mult),
              reads=[nbf.B], writes=[nbf.B])
        cwtok = sb("cwtok", [8, D])
        kb.dma("sp", out=cwtok.t[0:4, :], in_=I["conv_w"], writes=[cwtok.B])
        kb.dma("sp", out=cwtok.t[4:5, :], in_=I["conv_b"], writes=[cwtok.B])
        cw = sb("cw", [128, 8, 8])
        ps = kb.ps()

        def trcw(e):
            inst = None
            for c in range(8):
                inst = e.transpose(ps.t[:, c * 8:c * 8 + 5], cwtok.t[0:5, c * 128:(c + 1) * 128], cx.idf.t[0:5, 0:5])
            return inst
        kb.op("pe", trcw, reads=[cwtok.B, cx.idf.B], writes=[ps.B])
        kb.op("dve", lambda e: e.tensor_copy(out=cw.t[:, :, 0:5], in_=ps.t[:, 0:64].rearrange("p (c j) -> p c j", j=8)[:, :, 0:5]),
              reads=[ps.B], writes=[cw.B])
        b16 = sb("b16", [128, 16])
        kb.dma("sp", out=b16.t[:, 0:8], in_=rawap(I["b_f"], 0, [[0, 128], [1, 8]]), writes=[b16.B])
        kb.dma("sp", out=b16.t[:, 8:16], in_=rawap(I["dt_bias"], 0, [[0, 128], [1, 8]]), writes=[b16.B])
        ones8 = sb("ones8", [8, 512])
        kb.op("dve", lambda e: e.memset(ones8.t[:], 1.0), writes=[ones8.B])
        rawall = st.enter_context(nc.sbuf_tensor("rawall", [128, 8, 515], F32))
        rawB = [Buf() for _ in range(8)]

        if cx.cut == 1:
            kb.barrier()
            return
        hTs = sb("hTs", [128, 8, NS], BF16)
        kb.dma("sp", out=hTs.t[:], in_=Sx["h1T"].rearrange("(k q) n -> q k n", q=128)[:, :, NT:NT + NS], writes=[hTs.B])
        proj = sb("projs", [128, NIN])
        for cb in range(7):
            w = 512 if cb < 6 else NIN - 6 * 512
            col = cb * 512
            ps = kb.ps()
            kb.op("pe", lambda e: mmg(e, ps.t[0:NS, 0:w], [(hTs.t[:, k, :], win[:, k, col:col + w]) for k in range(8)]),
                  reads=winB + [hTs.B], writes=[ps.B])
            copy_op(kb, "act" if cb % 2 else "dve", proj.t[0:NS, col:col + w], ps.t[0:NS, 0:w], [ps.B], [proj.B])
        kb.dma("sp", out=O["k_sample"], in_=proj.t[32:64, 512:1024], reads=[proj.B], writes=[kb.B("o_ks")])
        kb.dma("sp", out=O["v_sample"], in_=proj.t[32:64, 1024:1536], reads=[proj.B], writes=[kb.B("o_vs")])
        kb.dma("sp", out=Sx["own"], in_=proj.t[64:68, :], reads=[proj.B], writes=[kb.B("own")])
        fs = sb("fs", [64, 8])
        kb.op("dve", lambda e: e.tensor_tensor(out=fs.t[32:64, :], in0=proj.t[32:64, 1536:1544], in1=b16.t[32:64, 0:8], op=ALU.add),
              reads=[proj.B, b16.B], writes=[fs.B])
        kb.op("act", lambda e: e.activation(out=fs.t[32:64, :], in_=fs.t[32:64, :], func=AF.Exp, scale=-1.0), reads=[fs.B], writes=[fs.B])
        kb.op("act", lambda e: e.activation(out=fs.t[32:64, :], in_=fs.t[32:64, :], func=AF.Ln, bias=cx.cst.t[32:64, 0:1], scale=1.0),
              reads=[fs.B, cx.cst.B], writes=[fs.B])
        kb.op("dve", lambda e: e.tensor_scalar(out=fs.t[32:64, :], in0=fs.t[32:64, :], scalar1=-1.0, scalar2=None, op0=ALU.mult),
              reads=[fs.B], writes=[fs.B])
        kb.dma("sp", out=O["logf_sample"], in_=fs.t[32:64, :], reads=[fs.B], writes=[kb.B("o_fs")])
        if cx.cut == 2:
            kb.barrier()
            return
        ps = kb.ps()

        def trh(e):
            inst = None
            for c in range(8):
                inst = e.transpose(ps.t[:, c * 4:c * 4 + 3], proj.t[0:3, 2056 + c * 128:2056 + (c + 1) * 128],
                                   cx.idf.t[0:3, 0:3])
            return inst
        kb.op("pe", trh, reads=[proj.B, cx.idf.B], writes=[ps.B])
        kb.op("dve", lambda e: e.tensor_copy(out=rawall[:, :, 0:3], in_=ps.t[:, 0:32].rearrange("p (c j) -> p c j", j=4)[:, :, 0:3]),
              reads=[ps.B], writes=rawB)
        if cx.cut == 3:
            kb.barrier()
            return
        ps = kb.ps()
        kb.op("pe", lambda e: mmg(e, ps.t[0:32, 0:200], [(hTs.t[:, k, 32:64], wc.t[:, k, 0:200]) for k in range(8)]),
              reads=[hTs.B, wc.B], writes=[ps.B])
        copy_op(kb, "dve", cx.qkv.t[0:32, 0:200], ps.t[0:32, 0:200], [ps.B], [cx.qkv.B])

        if cx.cut == 4:
            kb.barrier()
            return
        hTr = Ring([sb(f"hTp{i}", [128, 8, 512], BF16) for i in range(2)])
        efr = Ring([sb(f"ef{i}", [8, 512]) for i in range(2)])
        clr = Ring([sb(f"cl{i}", [8, 512]) for i in range(2)])
        clqr = Ring([sb(f"clq{i}", [8, 512], BF16) for i in range(2)])
        qsr = Ring([sb(f"qs{i}", [128, 512], BF16) for i in range(3)])
        accr = Ring([sb(f"acc{i}", [128, 512]) for i in range(2)])
        xcr = Ring([sb(f"xc{i}", [128, 512], BF16) for i in range(3)])
        xstok = sb("xstok", [128, 4, 512], BF16)
        bstok = sb("bstok", [128, 4, 256], BF16)
        kor = Ring([sb(f"ko{i}", [128, 512]) for i in range(2)])
        vor = Ring([sb(f"vo{i}", [128, 512]) for i in range(2)])
        vbr = Ring([sb(f"vb{i}", [128, 512], BF16) for i in range(2)])
        szr = Ring([sb(f"szt{i}", [128, 512], BF16) for i in range(2)])
        fdr = Ring([sb(f"fd{i}", [128, 16]) for i in range(2)])
        lfr = Ring([sb(f"lfo{i}", [128, 8]) for i in range(2)])
        clprev = None
        cnt = 0
        for tt in range(cx.NTT):
            c0 = tt * 512
            hT = hTr.next()
            kb.dma("sp", out=hT.t[:], in_=Sx["h1T"].rearrange("(k q) n -> q k n", q=128)[:, :, c0:c0 + 512], writes=[hT.B])
            ps = kb.ps()
            kb.op("pe", lambda e: mmg(e, ps.t[0:8, :], [(win[:, k, 1536:1544], hT.t[:, k, :]) for k in range(8)]),
                  reads=winB + [hT.B], writes=[ps.B])
            ef = efr.next()
            kb.op("act", lambda e: e.activation(out=ef.t[:], in_=ps.t[0:8, :], func=AF.Exp, scale=-1.0, bias=nbf.t[0:8, 0:1]),
                  reads=[ps.B, nbf.B], writes=[ef.B])
            kb.op("act", lambda e: e.activation(out=ef.t[:], in_=ef.t[:], func=AF.Ln, bias=cx.cst.t[0:8, 0:1], scale=1.0),
                  reads=[ef.B, cx.cst.B], writes=[ef.B])
            cl = clr.next()
            init = 0.0 if clprev is None else clprev.t[0:8, 511:512]
            kb.op("dve", lambda e: e.tensor_tensor_scan(out=cl.t[:], data0=ones8.t[:], data1=ef.t[:], initial=init,
                                                        op0=ALU.mult, op1=ALU.add),
                  reads=[ef.B, ones8.B] + ([clprev.B] if clprev is not None else []), writes=[cl.B])
            clprev = cl
            kb.dma("sp", out=Sx["CL"][:, c0:c0 + 512], in_=cl.t[:], reads=[cl.B], writes=[kb.B("CL", tt)])
            clq = clqr.next()
            kb.op("dve", lambda e: e.tensor_scalar(out=clq.t[:], in0=cl.t[:], scalar1=-1.0 / SCALE, scalar2=None, op0=ALU.mult),
                  reads=[cl.B], writes=[clq.B])
            if cx.cut == 5:
                kb.barrier()
                return
            for h in range(8):
                ps = kb.ps()

                def fq(e):
                    mmg(e, ps.t[0:64, :], [(win[:, k, h * 64:(h + 1) * 64], hT.t[:, k, :]) for k in range(8)])
                    return e.matmul(ps.t[64:65, :], lhsT=cx.idb.t[0:8, h:h + 1], rhs=clq.t[0:8, :], start=True, stop=True)
                kb.op("pe", fq, reads=winB + [hT.B, clq.B, cx.idb.B], writes=[ps.B])
                qs = qsr.next()
                cnt += 1
                copy_op(kb, "act" if cnt % 2 else "dve", qs.t[0:65, :], ps.t[0:65, :], [ps.B], [qs.B])
                kb.dma("sp", out=Sx["QT"][h * 65:(h + 1) * 65, c0:c0 + 512], in_=qs.t[0:65, :], reads=[qs.B],
                       writes=[kb.B("QT", tt, h)])
            for h in range(8):
                ps = kb.ps()
                kb.op("pe", lambda e: mmg(e, ps.t[0:64, :], [(win[:, k, 512 + h * 64:512 + (h + 1) * 64], hT.t[:, k, :]) for k in range(8)]),
                      reads=winB + [hT.B], writes=[ps.B])
                qs = qsr.next()
                cnt += 1
                copy_op(kb, "act" if cnt % 2 else "dve", qs.t[0:64, :], ps.t[0:64, :], [ps.B], [qs.B])
                kb.dma("sp", out=Sx["KT"][h * 64:(h + 1) * 64, c0:c0 + 512], in_=qs.t[0:64, :], reads=[qs.B],
                       writes=[kb.B("KT", tt, h)])
            if cx.cut == 6:
                kb.barrier()
                return
            for c in range(8):
                ps = kb.ps()
                kb.op("pe", lambda e: mmg(e, ps.t[:, :], [(win[:, k, 2056 + c * 128:2056 + (c + 1) * 128], hT.t[:, k, :]) for k in range(8)]),
                      reads=winB + [hT.B], writes=[ps.B])
                copy_op(kb, "act", rawall[:, c, 3:515], ps.t[:, :], [ps.B], [rawB[c]])
                acc = accr.next()
                kb.op("dve", lambda e: e.tensor_scalar(out=acc.t[:], in0=rawall[:, c, 0:512], scalar1=cw.t[:, c, 0:1],
                                                       scalar2=cw.t[:, c, 4:5], op0=ALU.mult, op1=ALU.add),
                      reads=[rawB[c], cw.B], writes=[acc.B])
                for j in range(1, 4):
                    kb.op("dve", lambda e: e.scalar_tensor_tensor(out=acc.t[:], in0=rawall[:, c, j:j + 512], scalar=cw.t[:, c, j:j + 1],
                                                                  in1=acc.t[:], op0=ALU.mult, op1=ALU.add),
                          reads=[rawB[c], cw.B, acc.B], writes=[acc.B])
                xc = xcr.next()
                kb.op("act", lambda e: e.activation(out=xc.t[:], in_=acc.t[:], func=AF.Silu), reads=[acc.B], writes=[xc.B])
                if tt == cx.NTT - 1:
                    pass
                else:
                    kb.op("pool", lambda e: e.tensor_copy(out=rawall[:, c, 0:3], in_=rawall[:, c, 512:515]),
                          reads=[rawB[c]], writes=[rawB[c]])
                if c < 6:
                    ps2 = kb.ps()
                    psb = ps2.t[:].bitcast(BF16)

                    def trx(e):
                        inst = None
                        for s in range(4):
                            inst = e.transpose(psb[:, s * 128:(s + 1) * 128], xc.t[:, s * 128:(s + 1) * 128], cx.idb.t[:])
                        return inst
                    kb.op("pe", trx, reads=[xc.B, cx.idb.B], writes=[ps2.B])
                    src = psb[:, 0:512].rearrange("p (s c) -> p s c", c=128)
                    if c < 4:
                        copy_op(kb, "dve", xstok.t[:, :, c * 128:(c + 1) * 128], src, [ps2.B], [xstok.B])
                    else:
                        copy_op(kb, "dve", bstok.t[:, :, (c - 4) * 128:(c - 3) * 128], src, [ps2.B], [bstok.B])
                if c == 3:
                    kb.dma("sp", out=Sx["xs"].rearrange("(s p) c -> p s c", p=128)[:, tt * 4:(tt + 1) * 4, :], in_=xstok.t[:],
                           reads=[xstok.B], writes=[kb.B("xs", tt)])
                if c == 5:
                    kb.dma("sp", out=Sx["Btok"].rearrange("(s p) c -> p s c", p=128)[:, tt * 4:(tt + 1) * 4, :], in_=bstok.t[:],
                           reads=[bstok.B], writes=[kb.B("Btok", tt)])
                if c >= 4:
                    dst = Sx["BT"] if c < 6 else Sx["CT"]
                    rr = (c - 4) * 128 if c < 6 else (c - 6) * 128
                    kb.dma("sp", out=dst[rr:rr + 128, c0:c0 + 512], in_=xc.t[:], reads=[xc.B], writes=[kb.B("BCT", tt, c)])
            if cx.cut == 7:
                kb.barrier()
                return
            if tt == cx.NTT - 1:
                ps = kb.ps()

                def trcp(e):
                    inst = None
                    for c in range(8):
                        inst = e.transpose(ps.t[0:3, (c % 4) * 128:(c % 4 + 1) * 128], rawall[:, c, 512:515], cx.idf.t[:, :])
                        if c == 3:
                            pass
                    return inst
                cps = sb("cps", [4, D])
                for half in range(2):
                    ps = kb.ps()

                    def trcp(e):
                        inst = None
                        for c in range(4):
                            inst = e.transpose(ps.t[0:3, c * 128:(c + 1) * 128], rawall[:, half * 4 + c, 512:515], cx.idf.t[:, :])
                        return inst
                    kb.op("pe", trcp, reads=rawB + [cx.idf.B], writes=[ps.B])
                    copy_op(kb, "dve", cps.t[0:3, half * 512:(half + 1) * 512], ps.t[0:3, :], [ps.B], [cps.B])
                kb.dma("sp", out=O["conv_prompt"], in_=cps.t[0:3, :], reads=[cps.B], writes=[kb.B("o_cp")])
            if cx.cut == 8:
                kb.barrier()
                return
            for s in range(4):
                r0 = c0 + s * 128
                lhs = [hT.t[:, k, s * 128:(s + 1) * 128] for k in range(8)]
                ps = kb.ps()
                kb.op("pe", lambda e: mmg(e, ps.t[:, :], [(lhs[k], win[:, k, 512:1024]) for k in range(8)]),
                      reads=winB + [hT.B], writes=[ps.B])
                ko = kor.next()
                copy_op(kb, "act", ko.t[:], ps.t[:, :], [ps.B], [ko.B])
                kb.dma("sp", out=O["k_prompt"][r0:r0 + 128, :], in_=ko.t[:], reads=[ko.B], writes=[kb.B("o_k", r0)])
                if cx.cut == 9:
                    kb.barrier()
                    return
                ps = kb.ps()
                kb.op("pe", lambda e: mmg(e, ps.t[:, :], [(lhs[k], win[:, k, 1024:1536]) for k in range(8)]),
                      reads=winB + [hT.B], writes=[ps.B])
                vo, vb = vor.next(), vbr.next()
                copy_op(kb, "act", vo.t[:], ps.t[:, :], [ps.B], [vo.B])
                copy_op(kb, "dve", vb.t[:], ps.t[:, :], [ps.B], [vb.B])
                kb.dma("sp", out=O["v_prompt"][r0:r0 + 128, :], in_=vo.t[:], reads=[vo.B], writes=[kb.B("o_v", r0)])
                kb.dma("sp", out=Sx["Vb"][r0:r0 + 128, :], in_=vb.t[:], reads=[vb.B], writes=[kb.B("Vb", r0)])
                if cx.cut == 10:
                    kb.barrier()
                    return
                ps = kb.ps()
                kb.op("pe", lambda e: mmg(e, ps.t[:, :], [(lhs[k], win[:, k, 1544:2056]) for k in range(8)]),
                      reads=winB + [hT.B], writes=[ps.B])
                szt = szr.next()
                kb.op("act", lambda e: e.activation(out=szt.t[:], in_=ps.t[:, :], func=AF.Silu), reads=[ps.B], writes=[szt.B])
                kb.dma("sp", out=Sx["sz"][r0:r0 + 128, :], in_=szt.t[:], reads=[szt.B], writes=[kb.B("sz", r0)])
                if cx.cut == 11:
                    kb.barrier()
                    return
                ps = kb.ps()

                def ffd(e):
                    mmg(e, ps.t[:, 0:8], [(lhs[k], win[:, k, 1536:1544]) for k in range(8)])
                    return mmg(e, ps.t[:, 8:16], [(lhs[k], win[:, k, 3080:3088]) for k in range(8)])
                kb.op("pe", ffd, reads=winB + [hT.B], writes=[ps.B])
                fd = fdr.next()
                kb.op("dve", lambda e: e.tensor_tensor(out=fd.t[:], in0=ps.t[:, 0:16], in1=b16.t[:], op=ALU.add),
                      reads=[ps.B, b16.B], writes=[fd.B])
                kb.op("act", lambda e: e.activation(out=fd.t[:, 0:8], in_=fd.t[:, 0:8], func=AF.Exp, scale=-1.0), reads=[fd.B], writes=[fd.B])
                kb.op("act", lambda e: e.activation(out=fd.t[:, 8:16], in_=fd.t[:, 8:16], func=AF.Exp), reads=[fd.B], writes=[fd.B])
                kb.op("act", lambda e: e.activation(out=fd.t[:], in_=fd.t[:], func=AF.Ln, bias=cx.cst.t[:, 0:1], scale=1.0),
                      reads=[fd.B, cx.cst.B], writes=[fd.B])
                lfo = lfr.next()
                kb.op("dve", lambda e: e.tensor_scalar(out=lfo.t[:], in0=fd.t[:, 0:8], scalar1=-1.0, scalar2=None, op0=ALU.mult),
                      reads=[fd.B], writes=[lfo.B])
                kb.dma("sp", out=O["logf_prompt"][r0:r0 + 128, :], in_=lfo.t[:], reads=[lfo.B], writes=[kb.B("o_lf", r0)])
                kb.dma("sp", out=Sx["dt"][r0:r0 + 128, :], in_=fd.t[:, 8:16], reads=[fd.B], writes=[kb.B("dt", r0)])
        kb.barrier()


def phase_ssd(cx, passno):
    kb, nc, I, Sx, O = cx.kb, cx.nc, cx.I, cx.S, cx.O
    NT = cx.NT
    NCH = NT // 128
    with_y = passno == 2
    tg = f"s{passno}"
    with ExitStack() as st:
        sb = lambda name, shape, dt=F32: cx.sb(name + tg, shape, dt, st)
        tri, ones, mneg = sb("tri", [128, 128]), sb("onesf", [128, 128]), sb("mnegf", [128, 128])
        kb.dma("sp", out=tri.t[:], in_=I["cst_tri"], writes=[tri.B])
        kb.dma("sp", out=mneg.t[:], in_=I["cst_maskneg"][:, 0, 0:128], writes=[mneg.B])
        kb.op("dve", lambda e: e.memset(ones.t[:], 1.0), writes=[ones.B])
        Ab = sb("Ab", [128, 8])
        kb.dma("sp", out=Ab.t[:], in_=rawap(I["a_log"], 0, [[0, 128], [1, 8]]), writes=[Ab.B])
        kb.op("act", lambda e: e.activation(out=Ab.t[:], in_=Ab.t[:], func=AF.Exp), reads=[Ab.B], writes=[Ab.B])
        kb.op("dve", lambda e: e.tensor_scalar(out=Ab.t[:], in0=Ab.t[:], scalar1=-1.0, scalar2=None, op0=ALU.mult),
              reads=[Ab.B], writes=[Ab.B])
        d8 = sb("d8", [128, 8])
        kb.dma("sp", out=d8.t[:], in_=rawap(I["d_skip"], 0, [[0, 128], [1, 8]]), writes=[d8.B])
        Db = sb("Db", [128, 8, 64])
        kb.op("dve", lambda e: e.tensor_copy(out=Db.t[:], in_=d8.t[:].unsqueeze(2).to_broadcast([128, 8, 64])),
              reads=[d8.B], writes=[Db.B])
        hs, hsb = sb("hs", [128, 512]), sb("hsb", [128, 512], BF16)
        if passno == 1:
            kb.op("dve", lambda e: e.memset(hs.t[:], 0.0), writes=[hs.B])
        else:
            kb.dma("sp", out=hs.t[:], in_=Sx["st1g"][0:128, :], writes=[hs.B])
            kb.op("dve", lambda e: e.tensor_scalar(out=hs.t[:], in0=hs.t[:], scalar1=cx.pc.t[:, 1:2], scalar2=None, op0=ALU.mult),
                  reads=[hs.B, cx.pc.B], writes=[hs.B])
        kb.op("dve", lambda e: e.tensor_copy(out=hsb.t[:], in_=hs.t[:]), reads=[hs.B], writes=[hsb.B])
        xsr = Ring([sb(f"xs{i}", [128, 8, 64], BF16) for i in range(2)])
        btr = Ring([sb(f"bt{i}", [128, 256], BF16) for i in range(2)])
        dtr = Ring([sb(f"dt{i}", [128, 8]) for i in range(2)])
        ar = Ring([sb(f"a{i}", [128, 8]) for i in range(2)])
        acr = Ring([sb(f"ac{i}", [128, 24]) for i in range(2)])
        exr = Ring([sb(f"ex{i}", [128, 24]) for i in range(2)])
        xdtr = Ring([sb(f"xdt{i}", [128, 8, 64], BF16) for i in range(2)])
        xddr = Ring([sb(f"xdd{i}", [128, 8, 64], BF16) for i in range(2)])
        if with_y:
            BTr = Ring([sb(f"BT{i}", [128, 2, 128], BF16) for i in range(2)])
            CTr = Ring([sb(f"CT{i}", [128, 2, 128], BF16) for i in range(2)])
            arep = sb("arep", [128, 8, 128])
            Lm = sb("Lm", [128, 8, 128])
            sc = sb("sc", [128, 8, 128], BF16)
            t1r = Ring([sb(f"t1{i}", [128, 8, 64]) for i in range(2)])
            t3r = Ring([sb(f"t3{i}", [128, 8, 64]) for i in range(2)])
            yr = Ring([sb(f"y{i}", [128, 8, 64]) for i in range(2)])
        for ci in range(NCH):
            r0 = ci * 128
            xs_t, B_t, dt_t = xsr.next(), btr.next(), dtr.next()
            kb.dma("sp", out=xs_t.t[:].rearrange("p h d -> p (h d)"), in_=Sx["xs"][r0:r0 + 128, :], writes=[xs_t.B])
            kb.dma("sp", out=B_t.t[:], in_=Sx["Btok"][r0:r0 + 128, :], writes=[B_t.B])
            kb.dma("sp", out=dt_t.t[:], in_=Sx["dt"][r0:r0 + 128, :], writes=[dt_t.B])
            if with_y:
                BT_t, CT_t = BTr.next(), CTr.next()
                kb.dma("sp", out=BT_t.t[:], in_=Sx["BT"].rearrange("(g n) t -> n g t", n=128)[:, :, r0:r0 + 128], writes=[BT_t.B])
                kb.dma("sp", out=CT_t.t[:], in_=Sx["CT"].rearrange("(g n) t -> n g t", n=128)[:, :, r0:r0 + 128], writes=[CT_t.B])
            a = ar.next()
            kb.op("dve", lambda e: e.tensor_tensor(out=a.t[:], in0=dt_t.t[:], in1=Ab.t[:], op=ALU.mult), reads=[dt_t.B, Ab.B], writes=[a.B])
            ps1 = kb.ps()

            def f1(e):
                e.matmul(ps1.t[:, 0:8], lhsT=tri.t[:], rhs=a.t[:], start=True, stop=True)
                return e.matmul(ps1.t[:, 8:16], lhsT=ones.t[:], rhs=a.t[:], start=True, stop=True)
            kb.op("pe", f1, reads=[tri.B, ones.B, a.B], writes=[ps1.B])
            ac, ex = acr.next(), exr.next()
            kb.op("dve", lambda e: e.tensor_copy(out=ac.t[:, 0:16], in_=ps1.t[:, 0:16]), reads=[ps1.B], writes=[ac.B])
            kb.op("dve", lambda e: e.tensor_tensor(out=ex.t[:, 8:16], in0=ac.t[:, 8:16], in1=ac.t[:, 0:8], op=ALU.subtract),
                  reads=[ac.B], writes=[ex.B])
            kb.op("dve", lambda e: e.tensor_scalar(out=ac.t[:, 16:24], in0=ac.t[:, 0:8], scalar1=-1.0, scalar2=None, op0=ALU.mult),
                  reads=[ac.B], writes=[ac.B])
            kb.op("act", lambda e: e.activation(out=ex.t[:, 0:8], in_=ac.t[:, 0:8], func=AF.Exp), reads=[ac.B], writes=[ex.B])
            kb.op("act", lambda e: e.activation(out=ex.t[:, 8:16], in_=ex.t[:, 8:16], func=AF.Exp), reads=[ex.B], writes=[ex.B])
            kb.op("act", lambda e: e.activation(out=ex.t[:, 16:24], in_=ac.t[:, 8:16], func=AF.Exp), reads=[ac.B], writes=[ex.B])
            xdt, xdd = xdtr.next(), xddr.next()
            kb.op("dve", lambda e: e.tensor_tensor(out=xdt.t[:], in0=xs_t.t[:], in1=dt_t.t[:].unsqueeze(2).to_broadcast([128, 8, 64]), op=ALU.mult),
                  reads=[xs_t.B, dt_t.B], writes=[xdt.B])
            kb.op("pool", lambda e: e.tensor_tensor(out=xdd.t[:], in0=xdt.t[:], in1=ex.t[:, 8:16].unsqueeze(2).to_broadcast([128, 8, 64]), op=ALU.mult),
                  reads=[xdt.B, ex.B], writes=[xdd.B])
            if with_y:
                kb.op("pool", lambda e: e.tensor_copy(out=arep.t[:], in_=a.t[:].unsqueeze(2).to_broadcast([128, 8, 128])),
                      reads=[a.B], writes=[arep.B])
                psL = [kb.ps(), kb.ps()]
                for hb in range(2):
                    def fl(e):
                        inst = None
                        for hh in range(4):
                            h = hb * 4 + hh
                            reg = psL[hb].t[:, hh * 128:(hh + 1) * 128]
                            e.matmul(reg, lhsT=arep.t[:, h, :], rhs=tri.t[:], start=True, stop=False)
                            inst = e.matmul(reg, lhsT=cx.idf.t[:], rhs=mneg.t[:], start=False, stop=True)
                        return inst
                    kb.op("pe", fl, reads=[arep.B, tri.B, cx.idf.B, mneg.B], writes=[psL[hb].B])
                    for hh in range(4):
                        h = hb * 4 + hh
                        kb.op("act", lambda e: e.activation(out=Lm.t[:, h, :], in_=psL[hb].t[:, hh * 128:(hh + 1) * 128], func=AF.Exp,
                                                            bias=ac.t[:, 16 + h:17 + h], scale=1.0),
                              reads=[psL[hb].B, ac.B], writes=[Lm.B])
                psG = kb.ps()

                def fg(e):
                    inst = None
                    for g in range(2):
                        inst = e.matmul(psG.t[:, g * 128:(g + 1) * 128], lhsT=BT_t.t[:, g, :], rhs=CT_t.t[:, g, :], start=True, stop=True)
                    return inst
                kb.op("pe", fg, reads=[BT_t.B, CT_t.B], writes=[psG.B])
                for g in range(2):
                    kb.op("dve", lambda e: e.tensor_tensor(out=sc.t[:, 4 * g:4 * g + 4, :], in0=Lm.t[:, 4 * g:4 * g + 4, :],
                                                           in1=psG.t[:, g * 128:(g + 1) * 128].unsqueeze(1).to_broadcast([128, 4, 128]), op=ALU.mult),
                          reads=[Lm.B, psG.B], writes=[sc.B])
                psY, psO = kb.ps(), kb.ps()

                def fy(e):
                    inst = None
                    for h in range(8):
                        inst = e.matmul(psY.t[:, h * 64:(h + 1) * 64], lhsT=sc.t[:, h, :], rhs=xdt.t[:, h, :], start=True, stop=True)
                    return inst
                kb.op("pe", fy, reads=[sc.B, xdt.B], writes=[psY.B])

                def fo(e):
                    inst = None
                    for g in range(2):
                        inst = e.matmul(psO.t[:, g * 256:(g + 1) * 256], lhsT=CT_t.t[:, g, :], rhs=hsb.t[:, g * 256:(g + 1) * 256], start=True, stop=True)
                    return inst
                kb.op("pe", fo, reads=[CT_t.B, hsb.B], writes=[psO.B])
                t1, t3, yt = t1r.next(), t3r.next(), yr.next()
                kb.op("dve", lambda e: e.tensor_tensor(out=t1.t[:], in0=psO.t[:, :].rearrange("p (h d) -> p h d", d=64),
                                                       in1=ex.t[:, 0:8].unsqueeze(2).to_broadcast([128, 8, 64]), op=ALU.mult),
                      reads=[psO.B, ex.B], writes=[t1.B])
                kb.op("dve", lambda e: e.tensor_tensor(out=t1.t[:], in0=t1.t[:], in1=psY.t[:, :].rearrange("p (h d) -> p h d", d=64), op=ALU.add),
                      reads=[t1.B, psY.B], writes=[t1.B])
                kb.op("pool", lambda e: e.tensor_tensor(out=t3.t[:], in0=xs_t.t[:], in1=Db.t[:], op=ALU.mult), reads=[xs_t.B, Db.B], writes=[t3.B])
                kb.op("dve", lambda e: e.tensor_tensor(out=yt.t[:], in0=t1.t[:], in1=t3.t[:], op=ALU.add), reads=[t1.B, t3.B], writes=[yt.B])
                kb.dma("sp", out=Sx["yssm"][r0:r0 + 128, :], in_=yt.t[:].rearrange("p h d -> p (h d)"), reads=[yt.B], writes=[kb.B("yssm", r0)])
            psS = kb.ps()

            def fs_(e):
                inst = None
                for g in range(2):
                    inst = e.matmul(psS.t[:, g * 256:(g + 1) * 256], lhsT=B_t.t[:, g * 128:(g + 1) * 128],
                                    rhs=xdd.t[:, 4 * g:4 * g + 4, :].rearrange("p h d -> p (h d)"), start=True, stop=True)
                return inst
            kb.op("pe", fs_, reads=[B_t.B, xdd.B], writes=[psS.B])
            kb.op("dve", lambda e: e.tensor_tensor(out=hs.t[:].rearrange("p (h d) -> p h d", d=64), in0=hs.t[:].rearrange("p (h d) -> p h d", d=64),
                                                   in1=ex.t[:, 16:24].unsqueeze(2).to_broadcast([128, 8, 64]), op=ALU.mult),
                  reads=[hs.B, ex.B], writes=[hs.B])
            kb.op("dve", lambda e: e.tensor_tensor(out=hs.t[:], in0=hs.t[:], in1=psS.t[:, :], op=ALU.add), reads=[hs.B, psS.B], writes=[hs.B])
            kb.op("act", lambda e: e.copy(out=hsb.t[:], in_=hs.t[:]), reads=[hs.B], writes=[hsb.B])
        if passno == 1:
            kb.dma("sp", out=Sx["st1"], in_=hs.t[:], reads=[hs.B], writes=[kb.B("st1")])
            kb.allgather(Sx["st1"], Sx["st1g"], [[0, 1], [2, 3], [4, 5], [6, 7]], reads=[kb.B("st1")], writes=[kb.B("st1g")])
        else:
            ps = kb.ps()

            def ftr(e):
                inst = None
                for j in range(4):
                    inst = e.transpose(ps.t[:, j * 128:(j + 1) * 128], hs.t[:, j * 128:(j + 1) * 128], cx.idf.t[:])
                return inst
            kb.op("pe", ftr, reads=[hs.B, cx.idf.B], writes=[ps.B])
            stT = sb("stT", [128, 512])
            kb.op("dve", lambda e: e.tensor_copy(out=stT.t[:], in_=ps.t[:, :]), reads=[ps.B], writes=[stT.B])
            kb.dma("sp", out=O["ssm_prompt"].rearrange("(j q) n -> q j n", q=128), in_=stT.t[:].rearrange("q (j n) -> q j n", n=128),
                   reads=[stT.B], writes=[kb.B("o_ssm")])
        kb.barrier()


def phase_attn(cx):
    kb, nc, I, Sx, O = cx.kb, cx.nc, cx.I, cx.S, cx.O
    NT = cx.NT
    NB = NT // 128
    NQ = NT // 512
    pairs = [[0, 1], [2, 3], [4, 5], [6, 7]]
    kb.allgather(Sx["KT"], Sx["KTg"], pairs)
    kb.allgather(Sx["Vb"], Sx["Vbg"], pairs)
    kb.allgather(Sx["CL"], Sx["CLg"], pairs)
    with ExitStack() as st:
        sb = lambda name, shape, dt=F32: cx.sb(name, shape, dt, st)
        ccB = kb.B("ccdone")
        ccB.w = ("cc", kb.ccval)
        KT = sb("KTa", [65, 8, 2 * NT], BF16)
        kb.dma("sp", out=KT.t[0:64, :, 0:NT], in_=Sx["KTg"][0:512, :].rearrange("(h d) n -> d h n", d=64), reads=[ccB], writes=[KT.B])
        kb.dma("sp", out=KT.t[0:64, :, NT:2 * NT], in_=Sx["KT"].rearrange("(h d) n -> d h n", d=64), writes=[KT.B])
        kb.op("pool", lambda e: e.memset(KT.t[64:65, :, :], 1.0), writes=[KT.B])
        QT = sb("QTa", [65, 8, NT], BF16)
        kb.dma("sp", out=QT.t[:], in_=Sx["QT"].rearrange("(h r) n -> r h n", r=65), writes=[QT.B])
        VA = sb("VA", [128, 2 * NB, 8, 66], BF16)
        kb.op("pool", lambda e: e.memset(VA.t[:, :, :, 64:66], 1.0), writes=[VA.B])
        vst = sb("vst", [128, NB, 512], BF16)
        for hf, src in ((0, Sx["Vbg"][0:NT, :]), (1, Sx["Vb"])):
            kb.dma("sp", out=vst.t[:], in_=src.rearrange("(blk p) c -> p blk c", p=128), reads=[ccB], writes=[vst.B])
            kb.op("pool", lambda e: e.tensor_copy(out=VA.t[:, hf * NB:(hf + 1) * NB, :, 0:64],
                                                  in_=vst.t[:].rearrange("p b (h d) -> p b h d", d=64)),
                  reads=[vst.B], writes=[VA.B])
        biasK = sb("biasK", [128, 2 * NB, 8])
        cls = sb("cls", [8, NT])
        for hf, src in ((0, Sx["CLg"][0:8, :]), (1, Sx["CL"])):
            kb.dma("sp", out=cls.t[:], in_=src, reads=[ccB], writes=[cls.B])
            ps = kb.ps()

            def ftr(e):
                inst = None
                for blk in range(NB):
                    inst = e.transpose(ps.t[:, blk * 8:(blk + 1) * 8], cls.t[0:8, blk * 128:(blk + 1) * 128], cx.idf.t[0:8, 0:8])
                return inst
            kb.op("pe", ftr, reads=[cls.B, cx.idf.B], writes=[ps.B])
            kb.op("dve", lambda e: e.tensor_copy(out=biasK.t[:, hf * NB:(hf + 1) * NB, :],
                                                 in_=ps.t[:, 0:NB * 8].rearrange("p (b h) -> p b h", h=8)),
                  reads=[ps.B], writes=[biasK.B])
            if hf == 0:
                xrep = sb("xrep", [8, 128])
                kb.op("dve", lambda e: e.tensor_copy(out=xrep.t[:], in_=cls.t[0:8, NT - 1:NT].to_broadcast([8, 128])),
                      reads=[cls.B], writes=[xrep.B])
                ps2 = kb.ps()
                kb.op("pe", lambda e: e.matmul(ps2.t[:, 0:8], lhsT=xrep.t[:], rhs=cx.idf.t[0:8, 0:8], start=True, stop=True),
                      reads=[xrep.B, cx.idf.B], writes=[ps2.B])
                tot = sb("tot", [128, 8])
                kb.op("dve", lambda e: e.tensor_copy(out=tot.t[:], in_=ps2.t[:, 0:8]), reads=[ps2.B], writes=[tot.B])
        kb.op("dve", lambda e: e.tensor_tensor(out=biasK.t[:, 0:NB, :], in0=biasK.t[:, 0:NB, :],
                                               in1=tot.t[:].unsqueeze(1).to_broadcast([128, NB, 8]), op=ALU.subtract),
              reads=[biasK.B, tot.B], writes=[biasK.B])
        kb.op("dve", lambda e: e.tensor_scalar(out=biasK.t[:, 0:NB, :], in0=biasK.t[:, 0:NB, :], scalar1=cx.pc.t[:, 0:1], scalar2=None, op0=ALU.add),
              reads=[biasK.B, cx.pc.B], writes=[biasK.B])
        masks = sb("masks", [128, 4, 512], BF16)
        kb.dma("pool", out=masks.t[:], in_=I["cst_maskneg"], writes=[masks.B])
        Pr = Ring([sb(f"P{i}", [128, 512], BF16) for i in range(3)])
        recr = Ring([sb(f"rec{i}", [128, 4, 1]) for i in range(2)])
        astr = Ring([sb(f"ast{i}", [128, 4, 64]) for i in range(2)])
        for h in range(8):
            for j in range(NQ):
                psO = kb.ps_hold()
                pov = psO.t[:, 0:264].rearrange("p (q c) -> p q c", c=66)
                kblocks = [(0, b) for b in range(NB)] + [(1, b) for b in range(4 * j + 4)]
                for (own, b) in kblocks:
                    blk = own * NB + b
                    kcol = own * NT + b * 128
                    psS = kb.ps()
                    diag = own and b >= 4 * j

                    def fsm(e):
                        inst = e.matmul(psS.t[:, :], lhsT=KT.t[0:65, h, kcol:kcol + 128], rhs=QT.t[0:65, h, j * 512:(j + 1) * 512],
                                        start=True, stop=not diag)
                        if diag:
                            inst = e.matmul(psS.t[:, :], lhsT=cx.idb.t[:], rhs=masks.t[:, b - 4 * j, :], start=False, stop=True)
                        return inst
                    kb.op("pe", fsm, reads=[KT.B, QT.B, cx.idb.B, masks.B], writes=[psS.B])
                    P = Pr.next()
                    kb.op("act", lambda e: e.activation(out=P.t[:], in_=psS.t[:, :], func=AF.Exp, scale=SCALE, bias=biasK.t[:, blk, h:h + 1]),
                          reads=[psS.B, biasK.B], writes=[P.B])

                    def fpv(e):
                        inst = None
                        for qs in range(4):
                            if own and b > 4 * j + qs:
                                continue
                            inst = e.matmul(pov[:, qs, :], lhsT=P.t[:, qs * 128:(qs + 1) * 128], rhs=VA.t[:, blk, h, :],
                                            start=(own == 0 and b == 0), stop=(own == 1 and b == 4 * j + qs))
                        return inst
                    kb.op("pe", fpv, reads=[P.B, VA.B], writes=[psO.B])
                rec, ast = recr.next(), astr.next()
                kb.op("dve", lambda e: e.reciprocal(out=rec.t[:], in_=pov[:, :, 64:65]), reads=[psO.B], writes=[rec.B])
                kb.op("dve", lambda e: e.tensor_tensor(out=ast.t[:], in0=pov[:, :, 0:64], in1=rec.t[:].to_broadcast([128, 4, 64]), op=ALU.mult),
                      reads=[psO.B, rec.B], writes=[ast.B])
                kb.ps_release(psO)
                kb.dma("sp", out=Sx["att"][0:NT, :].rearrange("(blk p) c -> p blk c", p=128)[:, 4 * j:4 * j + 4, h * 64:(h + 1) * 64], in_=ast.t[:],
                       reads=[ast.B], writes=[kb.B("att", h, j)])
        kb.barrier()


def phase_mix(cx):
    kb, nc, I, Sx, O = cx.kb, cx.nc, cx.I, cx.S, cx.O
    NT = cx.NT
    with ExitStack() as st:
        sb = lambda name, shape, dt=F32: cx.sb(name, shape, dt, st)
        wout = st.enter_context(nc.sbuf_tensor("wout", [128, 8, D], BF16))
        woB = [Buf() for _ in range(8)]
        for k in range(8):
            kb.dma("pool", out=wout[:, k, :], in_=I["w_out"][k * 128:(k + 1) * 128, :], writes=[woB[k]])
        gA = load_gain(cx, st, I["att_out_norm"], 512, "gA")
        gS = load_gain(cx, st, I["ssm_out_norm"], 512, "gS")
        g2 = load_gain(cx, st, I["ffn2_norm"], D, "g2")
        rs = norm_rings(cx, st, "mx")
        attr = Ring([sb(f"at{i}", [128, 512]) for i in range(2)])
        ysr = Ring([sb(f"ys{i}", [128, 512]) for i in range(2)])
        szr = Ring([sb(f"sz{i}", [128, 512], BF16) for i in range(2)])
        x1r = Ring([sb(f"x1{i}", [128, D]) for i in range(2)])
        x2r = Ring([sb(f"x2{i}", [128, D]) for i in range(2)])
        gr = Ring([sb(f"g{i}", [128, 512]) for i in range(2)])
        mxr = Ring([sb(f"mx{i}", [128, D], BF16) for i in range(2)])
        mTr = Ring([sb(f"mT{i}", [128, 8, 128], BF16) for i in range(2)])
        str_ = Ring([sb(f"stm{i}", [128, 16]) for i in range(2)])
        hTr = Ring([sb(f"hTm{i}", [128, 8, 512], BF16) for i in range(2)])
        junk = sb("junkm", [128, 512], BF16)
        tiles = [(tt * 512, [(tt * 512 + s * 128, 128, tt * 512 + s * 128, tt * 512 + s * 128) for s in range(4)]) for tt in range(cx.NTT)]
        tiles.append((NT, [(NT, 4, NT + 64, NT)]))
        for (c0, subs) in tiles:
            hT = hTr.next()
            col = 0
            for (ra, p, rx1, rx2) in subs:
                at, ys, sz, x1t = attr.next(), ysr.next(), szr.next(), x1r.next()
                kb.dma("sp", out=at.t[:p, :], in_=Sx["att"][ra:ra + p, :], writes=[at.B])
                kb.dma("sp", out=ys.t[:p, :], in_=Sx["yssm"][ra:ra + p, :], writes=[ys.B])
                kb.dma("sp", out=sz.t[:p, :], in_=Sx["sz"][ra:ra + p, :], writes=[sz.B])
                kb.dma("sp", out=x1t.t[:p, :], in_=Sx["x1"][rx1:rx1 + p, :], writes=[x1t.B])
                stt, g, mx = str_.next(), gr.next(), mxr.next()
                kb.op("dve", lambda e: e.tensor_tensor(out=g.t[:p, :], in0=ys.t[:p, :], in1=sz.t[:p, :], op=ALU.mult), reads=[ys.B, sz.B], writes=[g.B])
                kb.op("act", lambda e: e.activation(out=junk.t[:p, :], in_=at.t[:p, :], func=AF.Square, accum_out=stt.t[:p, 0:1]),
                      reads=[at.B], writes=[junk.B, stt.B])
                for gg in range(2):
                    kb.op("act", lambda e: e.activation(out=junk.t[:p, 0:256], in_=g.t[:p, gg * 256:(gg + 1) * 256], func=AF.Square,
                                                        accum_out=stt.t[:p, 1 + gg:2 + gg]), reads=[g.B], writes=[junk.B, stt.B])
                kb.op("dve", lambda e: e.tensor_scalar(out=stt.t[:p, 4:5], in0=stt.t[:p, 0:1], scalar1=1.0 / 512, scalar2=EPS, op0=ALU.mult, op1=ALU.add),
                      reads=[stt.B], writes=[stt.B])
                kb.op("dve", lambda e: e.tensor_scalar(out=stt.t[:p, 5:7], in0=stt.t[:p, 1:3], scalar1=1.0 / 256, scalar2=EPS, op0=ALU.mult, op1=ALU.add),
                      reads=[stt.B], writes=[stt.B])
                kb.op("act", lambda e: e.activation(out=stt.t[:p, 8:11], in_=stt.t[:p, 4:7], func=AF.Sqrt), reads=[stt.B], writes=[stt.B])
                kb.op("dve", lambda e: e.reciprocal(out=stt.t[:p, 12:15], in_=stt.t[:p, 8:11]), reads=[stt.B], writes=[stt.B])
                kb.op("dve", lambda e: e.scalar_tensor_tensor(out=mx.t[:p, 0:512], in0=at.t[:p, :], scalar=stt.t[:p, 12:13], in1=gA.t[:p, :],
                                                              op0=ALU.mult, op1=ALU.mult), reads=[at.B, stt.B, gA.B], writes=[mx.B])
                for gg in range(2):
                    kb.op("dve", lambda e: e.scalar_tensor_tensor(out=mx.t[:p, 512 + gg * 256:512 + (gg + 1) * 256], in0=g.t[:p, gg * 256:(gg + 1) * 256],
                                                                  scalar=stt.t[:p, 13 + gg:14 + gg], in1=gS.t[:p, gg * 256:(gg + 1) * 256],
                                                                  op0=ALU.mult, op1=ALU.mult), reads=[g.B, stt.B, gS.B], writes=[mx.B])
                ps = kb.ps()
                psb = ps.t[:].bitcast(BF16)

                def tr(e):
                    inst = None
                    for k in range(8):
                        inst = e.transpose(psb[:, k * 128:k * 128 + p], mx.t[:p, k * 128:(k + 1) * 128], cx.idb.t[:p, :p])
                    return inst
                kb.op("pe", tr, reads=[mx.B, cx.idb.B], writes=[ps.B])
                mT = mTr.next()
                kb.op("act", lambda e: e.copy(out=mT.t[:, :, 0:p], in_=psb.rearrange("q (k c) -> q k c", c=128)[:, :, 0:p]),
                      reads=[ps.B], writes=[mT.B])
                x2t = x2r.next()
                for hf in range(2):
                    ps = kb.ps()
                    kb.op("pe", lambda e: mmg(e, ps.t[:p, :], [(mT.t[:, k, 0:p], wout[:, k, hf * 512:(hf + 1) * 512]) for k in range(8)]),
                          reads=woB + [mT.B], writes=[ps.B])
                    kb.op("dve", lambda e: e.tensor_tensor(out=x2t.t[:p, hf * 512:(hf + 1) * 512], in0=ps.t[:p, :], in1=x1t.t[:p, hf * 512:(hf + 1) * 512], op=ALU.add),
                          reads=[ps.B, x1t.B], writes=[x2t.B])
                kb.dma("sp", out=Sx["x2"][rx2:rx2 + p, :], in_=x2t.t[:p, :], reads=[x2t.B], writes=[kb.B("x2", rx2)])
                norm_to_T(cx, x2t, p, D, g2, rs, hT, col)
                col += p
            kb.dma("sp", out=Sx["h2T"].rearrange("(k q) n -> q k n", q=128)[:, :, c0:c0 + col], in_=hT.t[:, :, 0:col],
                   reads=[hT.B], writes=[kb.B("h2T", c0)])
        kb.barrier()


def phase_sample(cx):
    kb, nc, I, Sx, O = cx.kb, cx.nc, cx.I, cx.S, cx.O
    NT, NG = cx.NT, cx.NG
    with ExitStack() as st:
        sb = lambda name, shape, dt=F32: cx.sb("sm_" + name, shape, dt, st)
        own = sb("ownt", [4, NIN])
        kb.dma("sp", out=own.t[:], in_=Sx["own"], writes=[own.B])
        sct = sb("sconvt", [4, 3, D])
        kb.dma("sp", out=sct.t[:], in_=I["sconv"], writes=[sct.B])
        cwb = sb("cwb", [4, 5, D])
        kb.dma("sp", out=cwb.t[:, 0:4, :], in_=rawap(I["conv_w"], 0, [[0, 4], [D, 4], [1, D]]), writes=[cwb.B])
        kb.dma("sp", out=cwb.t[:, 4, :], in_=rawap(I["conv_b"], 0, [[0, 4], [1, D]]), writes=[cwb.B])
        kb.dma("sp", out=O["conv_sample"][:, 0:2, :], in_=sct.t[:, 1:3, :], reads=[sct.B], writes=[kb.B("o_cs0")])
        kb.dma("sp", out=O["conv_sample"][:, 2, :], in_=own.t[:, 2056:3080], reads=[own.B], writes=[kb.B("o_cs1")])
        acc, tmp4 = sb("acc2", [4, D]), sb("tmp4", [4, D])
        for j in range(4):
            src = sct.t[:, j, :] if j < 3 else own.t[:, 2056:3080]
            if j == 0:
                kb.op("dve", lambda e: e.tensor_tensor(out=acc.t[:], in0=src, in1=cwb.t[:, 0, :], op=ALU.mult), reads=[sct.B, cwb.B], writes=[acc.B])
                kb.op("dve", lambda e: e.tensor_tensor(out=acc.t[:], in0=acc.t[:], in1=cwb.t[:, 4, :], op=ALU.add), reads=[acc.B, cwb.B], writes=[acc.B])
            else:
                kb.op("dve", lambda e: e.tensor_tensor(out=tmp4.t[:], in0=src, in1=cwb.t[:, j, :], op=ALU.mult), reads=[sct.B, own.B, cwb.B], writes=[tmp4.B])
                kb.op("dve", lambda e: e.tensor_tensor(out=acc.t[:], in0=acc.t[:], in1=tmp4.t[:], op=ALU.add), reads=[acc.B, tmp4.B], writes=[acc.B])
        xc2 = sb("xc2", [4, D])
        kb.op("act", lambda e: e.activation(out=xc2.t[:], in_=acc.t[:], func=AF.Silu), reads=[acc.B], writes=[xc2.B])
        sm = sb("sm4", [4, 64])
        kb.dma("sp", out=sm.t[:, 0:8], in_=rawap(I["dt_bias"], 0, [[0, 4], [1, 8]]), writes=[sm.B])
        kb.dma("sp", out=sm.t[:, 8:16], in_=rawap(I["a_log"], 0, [[0, 4], [1, 8]]), writes=[sm.B])
        kb.dma("sp", out=sm.t[:, 32:40], in_=rawap(I["d_skip"], 0, [[0, 4], [1, 8]]), writes=[sm.B])
        kb.op("act", lambda e: e.activation(out=sm.t[:, 8:16], in_=sm.t[:, 8:16], func=AF.Exp), reads=[sm.B], writes=[sm.B])
        kb.op("dve", lambda e: e.tensor_tensor(out=sm.t[:, 16:24], in0=own.t[:, 3080:3088], in1=sm.t[:, 0:8], op=ALU.add), reads=[own.B, sm.B], writes=[sm.B])
        kb.op("act", lambda e: e.activation(out=sm.t[:, 16:24], in_=sm.t[:, 16:24], func=AF.Exp), reads=[sm.B], writes=[sm.B])
        kb.op("act", lambda e: e.activation(out=sm.t[:, 16:24], in_=sm.t[:, 16:24], func=AF.Ln, bias=cx.cst.t[0:4, 0:1], scale=1.0),
              reads=[sm.B, cx.cst.B], writes=[sm.B])
        kb.op("dve", lambda e: e.tensor_tensor(out=sm.t[:, 24:32], in0=sm.t[:, 16:24], in1=sm.t[:, 8:16], op=ALU.mult), reads=[sm.B], writes=[sm.B])
        kb.op("act", lambda e: e.activation(out=sm.t[:, 24:32], in_=sm.t[:, 24:32], func=AF.Exp, scale=-1.0), reads=[sm.B], writes=[sm.B])
        pk = sb("pk", [4, 2048])
        kb.op("dve", lambda e: e.tensor_tensor(out=pk.t[:, 0:512].rearrange("p (h d) -> p h d", d=64), in0=xc2.t[:, 0:512].rearrange("p (h d) -> p h d", d=64),
                                               in1=sm.t[:, 16:24].unsqueeze(2).to_broadcast([4, 8, 64]), op=ALU.mult), reads=[xc2.B, sm.B], writes=[pk.B])
        kb.op("dve", lambda e: e.tensor_copy(out=pk.t[:, 512:1024], in_=xc2.t[:, 512:1024]), reads=[xc2.B], writes=[pk.B])
        kb.op("dve", lambda e: e.tensor_copy(out=pk.t[:, 1024:1056].rearrange("p (h q) -> p h q", q=4),
                                             in_=sm.t[:, 24:32].unsqueeze(2).to_broadcast([4, 8, 4])), reads=[sm.B], writes=[pk.B])
        kb.dma("sp", out=Sx["ssd2"], in_=pk.t[:], reads=[pk.B], writes=[kb.B("ssd2")])
        s2 = [kb.B("ssd2")]
        hst = sb("hst", [128, 2048])
        kb.dma("sp", out=hst.t[:], in_=I["sssm"], writes=[hst.B])
        xdp, dep = sb("xdp", [128, 16]), sb("dep", [128, 1])
        Bp, Cp = sb("Bp", [128, 128]), sb("Cp", [128, 128])
        for b in range(4):
            kb.dma("sp", out=xdp.t[b * 32:(b + 1) * 32, :], in_=rawap(Sx["ssd2"], b * 2048, [[16, 32], [1, 16]]), reads=s2, writes=[xdp.B])
            kb.dma("sp", out=dep.t[b * 32:(b + 1) * 32, :], in_=rawap(Sx["ssd2"], b * 2048 + 1024, [[1, 32], [1, 1]]), reads=s2, writes=[dep.B])
            for g in range(2):
                p0 = b * 32 + g * 16
                kb.dma("sp", out=Bp.t[p0:p0 + 16, :], in_=rawap(Sx["ssd2"], b * 2048 + 512 + g * 128, [[0, 16], [1, 128]]), reads=s2, writes=[Bp.B])
                kb.dma("sp", out=Cp.t[p0:p0 + 16, :], in_=rawap(Sx["ssd2"], b * 2048 + 768 + g * 128, [[0, 16], [1, 128]]), reads=s2, writes=[Cp.B])
        tmpS, hn = sb("tmpS", [128, 16, 128]), sb("hn", [128, 2048])
        kb.op("dve", lambda e: e.tensor_tensor(out=tmpS.t[:], in0=xdp.t[:].unsqueeze(2).to_broadcast([128, 16, 128]),
                                               in1=Bp.t[:].unsqueeze(1).to_broadcast([128, 16, 128]), op=ALU.mult), reads=[xdp.B, Bp.B], writes=[tmpS.B])
        kb.op("dve", lambda e: e.scalar_tensor_tensor(out=hn.t[:], in0=hst.t[:], scalar=dep.t[:, 0:1], in1=tmpS.t[:].rearrange("p a n -> p (a n)"),
                                                      op0=ALU.mult, op1=ALU.add), reads=[hst.B, dep.B, tmpS.B], writes=[hn.B])
        kb.dma("sp", out=O["ssm_sample"], in_=hn.t[:], reads=[hn.B], writes=[kb.B("o_ss")])
        kb.op("dve", lambda e: e.tensor_tensor(out=tmpS.t[:], in0=hn.t[:].rearrange("p (a n) -> p a n", n=128),
                                               in1=Cp.t[:].unsqueeze(1).to_broadcast([128, 16, 128]), op=ALU.mult), reads=[hn.B, Cp.B], writes=[tmpS.B])
        yp = sb("yp", [128, 16])
        kb.op("dve", lambda e: e.tensor_reduce(out=yp.t[:], in_=tmpS.t[:], axis=AX.X, op=ALU.add), reads=[tmpS.B], writes=[yp.B])
        for b in range(4):
            kb.dma("sp", out=rawap(Sx["y2"], b * 512, [[16, 32], [1, 16]]), in_=yp.t[b * 32:(b + 1) * 32, :], reads=[yp.B], writes=[kb.B("y2")])
        y2t = sb("y2t", [4, 512])
        kb.dma("sp", out=y2t.t[:], in_=Sx["y2"], reads=[kb.B("y2")], writes=[y2t.B])
        kb.op("dve", lambda e: e.tensor_tensor(out=tmp4.t[:, 0:512].rearrange("p (h d) -> p h d", d=64), in0=xc2.t[:, 0:512].rearrange("p (h d) -> p h d", d=64),
                                               in1=sm.t[:, 32:40].unsqueeze(2).to_broadcast([4, 8, 64]), op=ALU.mult), reads=[xc2.B, sm.B], writes=[tmp4.B])
        kb.op("dve", lambda e: e.tensor_tensor(out=y2t.t[:], in0=y2t.t[:], in1=tmp4.t[:, 0:512], op=ALU.add), reads=[y2t.B, tmp4.B], writes=[y2t.B])
        kb.dma("sp", out=Sx["yssm"][NT:NT + 4, :], in_=y2t.t[:], reads=[y2t.B], writes=[kb.B("yssm_s")])
        sz2 = sb("sz2", [4, 512], BF16)
        kb.op("act", lambda e: e.activation(out=sz2.t[:], in_=own.t[:, 1544:2056], func=AF.Silu), reads=[own.B], writes=[sz2.B])
        kb.dma("sp", out=Sx["sz"][NT:NT + 4, :], in_=sz2.t[:], reads=[sz2.B], writes=[kb.B("sz_s")])

        if cx.cut == 21:
            kb.barrier()
            return
        qkv = cx.qkv
        q2 = sb("q2aug", [32, 66])
        sd = sb("sd", [32, 8])
        kb.op("dve", lambda e: e.memset(q2.t[:], 0.0), writes=[q2.B])
        kb.op("dve", lambda e: e.tensor_copy(out=q2.t[:, 0:64], in_=qkv.t[:, 0:64]), reads=[qkv.B], writes=[q2.B])
        kb.op("dve", lambda e: e.tensor_tensor(out=sd.t[:, 0:1], in0=qkv.t[:, 192:193], in1=cx.pc.t[0:32, 2:3], op=ALU.add), reads=[qkv.B, cx.pc.B], writes=[sd.B])
        kb.op("act", lambda e: e.activation(out=sd.t[:, 0:1], in_=sd.t[:, 0:1], func=AF.Exp, scale=-1.0), reads=[sd.B], writes=[sd.B])
        kb.op("act", lambda e: e.activation(out=sd.t[:, 0:1], in_=sd.t[:, 0:1], func=AF.Ln, bias=cx.cst.t[0:32, 0:1], scale=1.0), reads=[sd.B, cx.cst.B], writes=[sd.B])
        kb.op("dve", lambda e: e.tensor_scalar(out=q2.t[:, 64:65], in0=sd.t[:, 0:1], scalar1=-1.0, scalar2=None, op0=ALU.mult), reads=[sd.B], writes=[q2.B])
        qk = sb("qk", [32, 64])
        kb.op("dve", lambda e: e.tensor_tensor(out=qk.t[:], in0=qkv.t[:, 0:64], in1=qkv.t[:, 64:128], op=ALU.mult), reads=[qkv.B], writes=[qk.B])
        kb.op("dve", lambda e: e.tensor_reduce(out=sd.t[:, 2:3], in_=qk.t[:], axis=AX.X, op=ALU.add), reads=[qk.B], writes=[sd.B])
        kb.op("act", lambda e: e.activation(out=sd.t[:, 3:4], in_=sd.t[:, 2:3], func=AF.Exp, scale=SCALE), reads=[sd.B], writes=[sd.B])
        mbd = sb("mbd", [32, 32, 66])
        kb.op("dve", lambda e: e.memset(mbd.t[:], 0.0), writes=[mbd.B])
        kb.dma("sp", out=mbd.t[:, :, 0:65], in_=I["cst_maskbd"], reads=[mbd.B], writes=[mbd.B])
        BD = sb("BD", [32, 32, 66])
        kb.op("dve", lambda e: e.tensor_tensor(out=BD.t[:], in0=q2.t[:].unsqueeze(1).to_broadcast([32, 32, 66]), in1=mbd.t[:], op=ALU.mult),
              reads=[q2.B, mbd.B], writes=[BD.B])
        onesf = sb("onesfd", [128, 128])
        kb.op("dve", lambda e: e.memset(onesf.t[:], 1.0), writes=[onesf.B])
        Qrep = sb("Qrep", [128, 32, 66])
        BDv, Qv = BD.t[:].rearrange("p a c -> p (a c)"), Qrep.t[:].rearrange("p a c -> p (a c)")
        for col in range(0, 32 * 66, 512):
            w = min(512, 32 * 66 - col)
            ps = kb.ps()
            kb.op("pe", lambda e: e.matmul(ps.t[:, 0:w], lhsT=onesf.t[0:32, :], rhs=BDv[:, col:col + w], start=True, stop=True),
                  reads=[onesf.B, BD.B], writes=[ps.B])
            kb.op("dve", lambda e: e.tensor_copy(out=Qv[:, col:col + w], in_=ps.t[:, 0:w]), reads=[ps.B], writes=[Qrep.B])
        if cx.cut == 22:
            kb.barrier()
            return
        pti = sb("pti", [32, cx.NPGS], I32)
        kb.dma("sp", out=pti.t[:], in_=I["ptab"], writes=[pti.B])
        ptf = sb("ptf", [32, NG, 32])
        kb.op("dve", lambda e: e.tensor_copy(out=ptf.t[:].rearrange("p g l -> p (g l)"), in_=pti.t[:]), reads=[pti.B], writes=[ptf.B])
        R4 = sb("R4", [32, NG, 4, 32])
        kb.op("dve", lambda e: e.tensor_copy(out=R4.t[:], in_=ptf.t[:].unsqueeze(2).to_broadcast([32, NG, 4, 32])), reads=[ptf.B], writes=[R4.B])
        ps = kb.ps()

        def fidx(e):
            inst = None
            for g in range(NG):
                inst = e.matmul(ps.t[:, g * 32:(g + 1) * 32], lhsT=R4.t[:, g, :, :].rearrange("p r l -> p (r l)"), rhs=cx.idf.t[0:32, 0:32], start=True, stop=True)
            return inst
        kb.op("pe", fidx, reads=[R4.B, cx.idf.B], writes=[ps.B])
        idxf = sb("idxf", [128, NG * 32])
        kb.op("dve", lambda e: e.tensor_scalar(out=idxf.t[:], in0=ps.t[:, 0:NG * 32], scalar1=4.0, scalar2=cx.pc.t[:, 3:4], op0=ALU.mult, op1=ALU.add),
              reads=[ps.B, cx.pc.B], writes=[idxf.B])
        idx = sb("idx", [128, NG * 32], I32)
        kb.op("dve", lambda e: e.tensor_copy(out=idx.t[:], in_=idxf.t[:]), reads=[idxf.B], writes=[idx.B])
        if cx.cut == 23:
            kb.barrier()
            return
        selcol = sb("selcol", [128, 32, 32], BF16)
        kb.dma("pool", out=selcol.t[:].rearrange("p a c -> p (a c)"), in_=I["cst_selcol"], writes=[selcol.B])
        uord = sb("uord", [128, 128])
        kb.dma("sp", out=uord.t[:], in_=I["cst_uord"], writes=[uord.B])
        LF = sb("LF", [128, 32 * NG, 32])
        for b in range(32):
            for g in range(NG):
                c_ = g * 32 + b
                kb.dma("pool", fn=lambda e: e.indirect_dma_start(out=LF.t[:, b * NG + g, :], out_offset=None, in_=I["clf"],
                                                                 in_offset=bass.IndirectOffsetOnAxis(ap=idx.t[:, c_:c_ + 1], axis=0)),
                       reads=[idx.B], writes=[LF.B])
        if cx.cut == 24:
            kb.barrier()
            return
        Ktr = Ring([sb(f"Kt{i}", [128, 32, 64]) for i in range(2)])
        Vtr = Ring([sb(f"Vt{i}", [128, 32, 64]) for i in range(2)])
        VPr = Ring([sb(f"VP{i}", [128, 32, 64], BF16) for i in range(2)])
        tmpk = sb("tmpk", [128, 32, 64])
        RS = sb("RS", [128, 32 * NG])
        kb.op("dve", lambda e: e.memset(RS.t[:], 0.0), writes=[RS.B])
        ones32 = sb("ones32", [128, 32])
        kb.op("dve", lambda e: e.memset(ones32.t[:], 1.0), writes=[ones32.B])
        Tt, IP, lpq, base, rev = sb("Tt", [128, NG]), sb("IP", [128, NG, 32]), sb("lpq", [128, 2 * NG]), sb("base", [128, NG]), sb("rev", [128, NG, 32])
        lq = sb("lq", [128, NG])
        scr = Ring([sb(f"scd{i}", [128, 32]) for i in range(2)])
        Sr = Ring([sb(f"Sd{i}", [128, 32]) for i in range(2)])
        Pr = Ring([sb(f"Pd{i}", [128, 32]) for i in range(2)])
        psA = kb.ps_hold()
        steps = [(b, g) for b in range(32) for g in range(NG)]
        gath = {}

        def issue(i):
            b, g = steps[i]
            c_ = g * 32 + b
            Kt, Vt = Ktr.next(), Vtr.next()
            kb.dma("pool", fn=lambda e: e.indirect_dma_start(out=Kt.t[:].rearrange("p r d -> p (r d)"), out_offset=None, in_=I["ck"],
                                                             in_offset=bass.IndirectOffsetOnAxis(ap=idx.t[:, c_:c_ + 1], axis=0)),
                   reads=[idx.B], writes=[Kt.B])
            kb.dma("pool", fn=lambda e: e.indirect_dma_start(out=Vt.t[:].rearrange("p r d -> p (r d)"), out_offset=None, in_=I["cv"],
                                                             in_offset=bass.IndirectOffsetOnAxis(ap=idx.t[:, c_:c_ + 1], axis=0)),
                   reads=[idx.B], writes=[Vt.B])
            gath[i] = (Kt, Vt)
        issue(0)
        for i, (b, g) in enumerate(steps):
            if i + 1 < len(steps):
                issue(i + 1)
            if g == 0:
                lfb = LF.t[:, b * NG:(b + 1) * NG, :]
                kb.op("dve", lambda e: e.tensor_reduce(out=Tt.t[:], in_=lfb, axis=AX.X, op=ALU.add), reads=[LF.B], writes=[Tt.B])
                for gg in range(NG):
                    kb.op("dve", lambda e: e.tensor_tensor_scan(out=IP.t[:, gg, :], data0=ones32.t[:], data1=LF.t[:, b * NG + gg, :], initial=0.0,
                                                                op0=ALU.mult, op1=ALU.add), reads=[LF.B, ones32.B], writes=[IP.B])
                ps = kb.ps()

                def fb(e):
                    e.matmul(ps.t[:, 0:NG], lhsT=uord.t[:], rhs=Tt.t[:], start=True, stop=True)
                    return e.matmul(ps.t[:, NG:2 * NG], lhsT=onesf.t[:], rhs=Tt.t[:], start=True, stop=True)
                kb.op("pe", fb, reads=[uord.B, onesf.B, Tt.B], writes=[ps.B])
                kb.op("dve", lambda e: e.tensor_copy(out=lpq.t[:], in_=ps.t[:, 0:2 * NG]), reads=[ps.B], writes=[lpq.B])
                kb.op("dve", lambda e: e.memset(lq.t[:], 0.0), writes=[lq.B])
                for gg in range(NG - 2, -1, -1):
                    kb.op("dve", lambda e: e.tensor_tensor(out=lq.t[:, gg:gg + 1], in0=lq.t[:, gg + 1:gg + 2], in1=lpq.t[:, NG + gg + 1:NG + gg + 2], op=ALU.add),
                          reads=[lq.B, lpq.B], writes=[lq.B])
                kb.op("dve", lambda e: e.tensor_tensor(out=base.t[:], in0=Tt.t[:], in1=lpq.t[:, 0:NG], op=ALU.add), reads=[Tt.B, lpq.B], writes=[base.B])
                kb.op("dve", lambda e: e.tensor_tensor(out=base.t[:], in0=base.t[:], in1=lq.t[:], op=ALU.add), reads=[base.B, lq.B], writes=[base.B])
                kb.op("dve", lambda e: e.tensor_tensor(out=rev.t[:], in0=base.t[:].unsqueeze(2).to_broadcast([128, NG, 32]), in1=IP.t[:], op=ALU.subtract),
                      reads=[base.B, IP.B], writes=[rev.B])
            if cx.cut == 25 and i == 0:
                kb.ps_release(psA)
                kb.barrier()
                return
            Kt, Vt = gath.pop(i)
            kb.op("dve", lambda e: e.tensor_tensor(out=tmpk.t[:], in0=Kt.t[:], in1=Qrep.t[:, b, 0:64].unsqueeze(1).to_broadcast([128, 32, 64]), op=ALU.mult),
                  reads=[Kt.B, Qrep.B], writes=[tmpk.B])
            sc_, S_, P_ = scr.next(), Sr.next(), Pr.next()
            kb.op("dve", lambda e: e.tensor_reduce(out=sc_.t[:], in_=tmpk.t[:], axis=AX.X, op=ALU.add), reads=[tmpk.B], writes=[sc_.B])
            kb.op("dve", lambda e: e.scalar_tensor_tensor(out=S_.t[:], in0=sc_.t[:], scalar=SCALE, in1=rev.t[:, g, :], op0=ALU.mult, op1=ALU.add),
                  reads=[sc_.B, rev.B], writes=[S_.B])
            kb.op("act", lambda e: e.activation(out=P_.t[:], in_=S_.t[:], func=AF.Exp, bias=Qrep.t[:, b, 64:65], scale=1.0,
                                                accum_out=RS.t[:, b * NG + g:b * NG + g + 1]), reads=[S_.B, Qrep.B], writes=[P_.B, RS.B])
            VP = VPr.next()
            kb.op("pool", lambda e: e.tensor_tensor(out=VP.t[:], in0=Vt.t[:], in1=P_.t[:].unsqueeze(2).to_broadcast([128, 32, 64]), op=ALU.mult),
                  reads=[Vt.B, P_.B], writes=[VP.B])
            VPv = VP.t[:].rearrange("p r d -> p (r d)")

            def fa(e):
                inst = None
                for ch in range(4):
                    inst = e.matmul(psA.t[0:32, :], lhsT=selcol.t[:, b, :], rhs=VPv[:, ch * 512:(ch + 1) * 512],
                                    start=(i == 0 and ch == 0), stop=(i == len(steps) - 1 and ch == 3))
                return inst
            kb.op("pe", fa, reads=[selcol.B, VP.B], writes=[psA.B])
        if cx.cut == 26:
            kb.ps_release(psA)
            kb.barrier()
            return
        RSb = sb("RSb", [128, 32])
        kb.op("dve", lambda e: e.tensor_reduce(out=RSb.t[:], in_=RS.t[:].rearrange("p (b g) -> p b g", g=NG), axis=AX.X, op=ALU.add), reads=[RS.B], writes=[RSb.B])
        ps = kb.ps()
        kb.op("pe", lambda e: e.matmul(ps.t[0:32, 0:2], lhsT=RSb.t[:], rhs=onesf.t[:, 0:2], start=True, stop=True), reads=[RSb.B, onesf.B], writes=[ps.B])
        kb.op("dve", lambda e: e.tensor_tensor(out=sd.t[:, 5:6], in0=ps.t[0:32, 0:1], in1=sd.t[:, 3:4], op=ALU.add), reads=[ps.B, sd.B], writes=[sd.B])
        kb.op("dve", lambda e: e.reciprocal(out=sd.t[:, 6:7], in_=sd.t[:, 5:6]), reads=[sd.B], writes=[sd.B])
        a2 = sb("a2", [32, 64])
        kb.op("dve", lambda e: e.tensor_reduce(out=a2.t[:], in_=psA.t[0:32, :].rearrange("b (r d) -> b d r", d=64), axis=AX.X, op=ALU.add),
              reads=[psA.B], writes=[a2.B])
        kb.ps_release(psA)
        kb.op("dve", lambda e: e.scalar_tensor_tensor(out=a2.t[:], in0=qkv.t[:, 128:192], scalar=sd.t[:, 3:4], in1=a2.t[:], op0=ALU.mult, op1=ALU.add),
              reads=[qkv.B, sd.B, a2.B], writes=[a2.B])
        kb.op("dve", lambda e: e.tensor_scalar(out=a2.t[:], in0=a2.t[:], scalar1=sd.t[:, 6:7], scalar2=None, op0=ALU.mult), reads=[a2.B, sd.B], writes=[a2.B])
        kb.dma("sp", out=Sx["att2"], in_=a2.t[:], reads=[a2.B], writes=[kb.B("att2")])
        if cx.cut == 27:
            kb.barrier()
            return
        kb.allgather(Sx["att2"], Sx["att2g"], [[0, 1, 2, 3, 4, 5, 6, 7]], reads=[kb.B("att2")], writes=[kb.B("att2g")])
        Aall = sb("Aall", [32, 8, 64])
        kb.dma("sp", out=Aall.t[:], in_=Sx["att2g"].rearrange("(h b) d -> b h d", b=32), reads=[kb.B("att2g")], writes=[Aall.B])
        selt = sb("selt", [32, 4])
        kb.dma("sp", out=selt.t[:], in_=I["sel"], writes=[selt.B])
        ps = kb.ps()
        kb.op("pe", lambda e: e.matmul(ps.t[0:4, :], lhsT=selt.t[:], rhs=Aall.t[:].rearrange("p h d -> p (h d)"), start=True, stop=True),
              reads=[selt.B, Aall.B], writes=[ps.B])
        ao = sb("ao", [4, 512])
        kb.op("dve", lambda e: e.tensor_copy(out=ao.t[:], in_=ps.t[0:4, :]), reads=[ps.B], writes=[ao.B])
        kb.dma("sp", out=Sx["att"][NT:NT + 4, :], in_=ao.t[:], reads=[ao.B], writes=[kb.B("att_s")])
        kb.barrier()


def make_consts():
    c = {}
    c["cst_idf"] = np.eye(128, dtype=np.float32)
    s = np.arange(128)
    c["cst_tri"] = (s[:, None] <= s[None, :]).astype(np.float32)
    c["cst_ustrict"] = (s[:, None] > s[None, :]).astype(np.float32)
    q = np.arange(512)
    m = np.zeros((128, 4, 512), np.float32)
    for v in range(4):
        m[:, v, :] = np.where((s[:, None] + 128 * v) > q[None, :], NEG, 0.0)
    c["cst_maskneg"] = m
    bd = np.zeros((32, 32, 65), np.float32)
    for b in range(32):
        bd[b, b, :] = 1.0
    c["cst_maskbd"] = bd
    sc = np.zeros((128, 32, 32), np.float32)
    for b in range(32):
        sc[:, b, b] = 1.0
    c["cst_selcol"] = sc.reshape(128, 1024)
    o = (s % 32) * 4 + s // 32
    c["cst_uord"] = (o[:, None] > o[None, :]).astype(np.float32)
    return c


def make_inmaps(inp, NT):
    f32 = lambda a: np.ascontiguousarray(np.asarray(a, dtype=np.float32))
    consts = make_consts()
    xp = np.asarray(inp["x_prompt"])
    xsamp = np.asarray(inp["x_sample"])[:, 0, :]
    ck = np.asarray(inp["cache_k"])[0]
    cv = np.asarray(inp["cache_v"])[0]
    clf = np.asarray(inp["cache_logf"])[0]
    npool = ck.shape[0]
    w_in = np.asarray(inp["w_in"])[0]
    shared = {}
    for nm in ("ffn1_norm", "mix_norm", "ffn2_norm", "conv_b", "b_f", "dt_bias", "a_log", "d_skip",
               "att_out_norm", "ssm_out_norm"):
        shared[nm] = f32(np.asarray(inp[nm]).reshape(1, -1))
    shared["final_norm"] = f32(np.asarray(inp["final_norm"]).reshape(1, -1))
    for nm in ("w_ffn1_in", "w_ffn2_in", "w_ffn1_out", "w_ffn2_out", "w_out", "conv_w"):
        shared[nm] = f32(np.asarray(inp[nm])[0])
    shared["w_in"] = f32(w_in)
    shared["ptab"] = np.ascontiguousarray(np.asarray(inp["page_table"], dtype=np.int32))
    shared.update(consts)
    maps = []
    for c in range(8):
        b, s = c // 2, c % 2
        m = dict(shared)
        xin = np.zeros((NT + NS, D), np.float32)
        xin[:NT] = xp[b, s * NT:(s + 1) * NT]
        xin[NT + 32:NT + 64] = xsamp
        xin[NT + 64:NT + 68] = xsamp[4 * c:4 * c + 4]
        if s == 1:
            xin[NT:NT + 3] = xp[b, NT - 3:NT]
        m["xin"] = xin
        m["ck"] = np.ascontiguousarray(ck[:, :, c, :]).reshape(npool * 4, 2048)
        m["cv"] = np.ascontiguousarray(cv[:, :, c, :]).reshape(npool * 4, 2048)
        m["clf"] = np.ascontiguousarray(clf[:, :, c]).reshape(npool * 4, 32)
        m["sconv"] = f32(np.asarray(inp["state_conv"])[0, 4 * c:4 * c + 4])
        m["sssm"] = f32(np.asarray(inp["state_ssm"])[0, 4 * c:4 * c + 4]).reshape(128, 2048)
        wc = np.zeros((D, 200), np.float32)
        wc[:, 0:64] = w_in[:, c * 64:(c + 1) * 64]
        wc[:, 64:128] = w_in[:, 512 + c * 64:512 + (c + 1) * 64]
        wc[:, 128:192] = w_in[:, 1024 + c * 64:1024 + (c + 1) * 64]
        wc[:, 192] = w_in[:, 1536 + c]
        m["wc"] = wc
        pc = np.zeros((128, 8), np.float32)
        pc[:, 0] = NEG if s == 0 else 0.0
        pc[:, 1] = 0.0 if s == 0 else 1.0
        pc[:, 2] = np.asarray(inp["b_f"])[0, c]
        pc[:, 3] = np.arange(128) // 32 + 0.25
        m["percore"] = pc
        sel = np.zeros((32, 4), np.float32)
        for i in range(4):
            sel[4 * c + i, i] = 1.0
        m["sel"] = sel
        maps.append(m)
    return maps


def assemble(res, NT, SEQ):
    B = 4
    y_prompt = np.zeros((B, SEQ, D), np.float32)
    k_prompt = np.zeros((1, B, SEQ, 8, 64), np.float32)
    v_prompt = np.zeros((1, B, SEQ, 8, 64), np.float32)
    logf_prompt = np.zeros((1, B, SEQ, 8), np.float32)
    conv_prompt = np.zeros((1, B, 3, D), np.float32)
    ssm_prompt = np.zeros((1, B, 8, 64, 128), np.float32)
    y_sample = np.zeros((32, 1, D), np.float32)
    conv_sample = np.zeros((1, 32, 3, D), np.float32)
    ssm_sample = np.zeros((1, 32, 8, 64, 128), np.float32)
    for c in range(8):
        b, s = c // 2, c % 2
        r = res[c]
        sl = slice(s * NT, (s + 1) * NT)
        y_prompt[b, sl] = r["y_prompt"]
        k_prompt[0, b, sl] = r["k_prompt"].reshape(NT, 8, 64)
        v_prompt[0, b, sl] = r["v_prompt"].reshape(NT, 8, 64)
        logf_prompt[0, b, sl] = r["logf_prompt"]
        if s == 1:
            conv_prompt[0, b] = r["conv_prompt"]
            ssm_prompt[0, b] = r["ssm_prompt"].reshape(8, 64, 128)
        y_sample[4 * c:4 * c + 4, 0] = r["y_sample"]
        conv_sample[0, 4 * c:4 * c + 4] = r["conv_sample"]
        ssm_sample[0, 4 * c:4 * c + 4] = r["ssm_sample"].reshape(4, 8, 64, 128)
    k_sample = res[0]["k_sample"].reshape(1, 32, 1, 8, 64).copy()
    v_sample = res[0]["v_sample"].reshape(1, 32, 1, 8, 64).copy()
    logf_sample = res[0]["logf_sample"].reshape(1, 32, 1, 8).copy()
    return (y_prompt, y_sample, k_prompt, v_prompt, logf_prompt, conv_prompt, ssm_prompt,
            k_sample, v_sample, logf_sample, conv_sample, ssm_sample)


def run(inp, stages=99, debug=False, npool=None, cores=None, cut=99, skip=None):
    SEQ = np.asarray(inp["x_prompt"]).shape[1]
    NT = SEQ // 2
    NPGS = np.asarray(inp["page_table"]).shape[1]
    NPOOL = np.asarray(inp["cache_k"]).shape[1]
    if npool is not None:
        NPOOL = npool
        inp = dict(inp)
        for nm in ("cache_k", "cache_v", "cache_logf"):
            inp[nm] = np.asarray(inp[nm])[:, :npool]
    maps = make_inmaps(inp, NT)
    if cores is not None:
        nc = build(NT, NPGS, NPOOL, stages=stages, debug=debug, ncores=len(cores), cut=cut)
        res = run_bass_kernel_spmd(nc, [maps[c] for c in cores], core_ids=list(range(len(cores))))
        return {c: r for c, r in zip(cores, res.results)}, NT, SEQ
    nc = build(NT, NPGS, NPOOL, stages=stages, debug=debug, skip=skip, cut=cut)
    res = run_bass_kernel_spmd(nc, maps, core_ids=list(range(8)))
    return res.results, NT, SEQ


def kernel(**inputs):
    res, NT, SEQ = run(inputs)
    return assemble(res, NT, SEQ)
```
